# Optimizing a Trainium2 kernel written in Bass

```python
import math
import jax, jax.numpy as jnp
from jax import lax
import numpy as np

D_MODEL = 1024
BATCH = 8
SEQ = 4096
DEPTH = 1
DEC_BATCH = 16
DEC_SEQ = 64
PAST_LEN = 2048

CHUNK = 64
Q_BLOCK = 128
HA = 8
DA = 64
HB = 16
DB = 64
LEFT_CHUNKS = 8
BAND_ROWS = LEFT_CHUNKS * CHUNK
MAX_REL = 128
T5_BUCKETS = 32
T5_MAX_DIST = 128
D_FF = 2816
CONV_W = 3
ALPHA = (2.0 * DEPTH) ** 0.25
BETA = (8.0 * DEPTH) ** -0.25
LN_EPS = 1e-5
RMS_EPS = 1e-5
NEG = -1e30

WA = HA * 2 * DA
WB = HB * DB
N_IN = 3 * WA + 3 * WB + 2 * D_MODEL

kernel_name = "hybrid_diffattn_chunkband_convffn_step"


def layer_norm(x, g, b):
    xf = x.astype(jnp.float32)
    mu = jnp.mean(xf, -1, keepdims=True)
    var = jnp.mean(jnp.square(xf - mu), -1, keepdims=True)
    y = (xf - mu) * lax.rsqrt(var + LN_EPS) * g.astype(jnp.float32) + b.astype(jnp.float32)
    return y.astype(x.dtype)


def head_rmsnorm(o, g):
    of = o.astype(jnp.float32)
    of = of * lax.rsqrt(jnp.mean(of * of, -1, keepdims=True) + RMS_EPS)
    return (of * g.astype(jnp.float32)).astype(o.dtype)


def t5_bucket(rel):
    nb = T5_BUCKETS // 2
    max_exact = nb // 2
    ret = jnp.where(rel > 0, nb, 0)
    n = jnp.abs(rel)
    nf = jnp.maximum(n, 1).astype(jnp.float32)
    large = max_exact + (jnp.log(nf / max_exact) / math.log(T5_MAX_DIST / max_exact)
                         * (nb - max_exact)).astype(jnp.int32)
    large = jnp.minimum(large, nb - 1)
    return ret + jnp.where(n < max_exact, n, large)


def split_proj(x, w_in):
    B, T = x.shape[:2]
    p = jnp.einsum('btd,dn->btn', x, w_in)
    cuts = [WA, 2 * WA, 3 * WA, 3 * WA + WB, 3 * WA + 2 * WB, 3 * WA + 3 * WB, 3 * WA + 3 * WB + D_MODEL]
    qa, ka, va, qb, kb, vb, ga, gb = jnp.split(p, cuts, axis=-1)
    qa = qa.reshape(B, T, HA, 2, DA)
    ka = ka.reshape(B, T, HA, 2, DA)
    va = va.reshape(B, T, HA, 2 * DA)
    qb = qb.reshape(B, T, HB, DB)
    kb = kb.reshape(B, T, HB, DB)
    vb = vb.reshape(B, T, HB, DB)
    return qa, ka, va, qb, kb, vb, ga, gb


def diff_attention(q, k, v, qpos, kpos, lam, t5_table):
    s = jnp.einsum('bqhmd,bkhmd->bhmqk', q, k).astype(jnp.float32) * (DA ** -0.5)
    rel = kpos[None, :] - qpos[:, None]
    bias = jnp.transpose(t5_table[t5_bucket(rel)].astype(jnp.float32), (2, 0, 1))
    mask = (kpos[None, :] // CHUNK) <= (qpos[:, None] // CHUNK)
    s = jnp.where(mask, s + bias[None, :, None], NEG)
    p = jax.nn.softmax(s, axis=-1)
    w = p[:, :, 0] - lam * p[:, :, 1]
    return jnp.einsum('bhqk,bkhe->bqhe', w.astype(v.dtype), v)


def band_attention(q, k, v, qpos, kpos, rel_table):
    s = jnp.einsum('bqhd,bkhd->bhqk', q, k).astype(jnp.float32) * (DB ** -0.5)
    rel = jnp.clip(kpos[None, :] - qpos[:, None], -MAX_REL, MAX_REL) + MAX_REL
    bias = jnp.transpose(rel_table[rel].astype(jnp.float32), (2, 0, 1))
    qc = qpos[:, None] // CHUNK
    kc = kpos[None, :] // CHUNK
    mask = (kpos[None, :] >= 0) & (kc <= qc) & (kc >= qc - LEFT_CHUNKS)
    s = jnp.where(mask, s + bias[None], NEG)
    p = jax.nn.softmax(s, axis=-1)
    return jnp.einsum('bhqk,bkhd->bqhd', p.astype(v.dtype), v)


def prompt_diff(qa, ka, va, lam, t5_table):
    B, S = qa.shape[:2]
    nblk = S // Q_BLOCK
    kpos = jnp.arange(S)
    qblocks = jnp.moveaxis(qa.reshape(B, nblk, Q_BLOCK, HA, 2, DA), 1, 0)

    def one_block(args):
        qblk, i = args
        qpos = i * Q_BLOCK + jnp.arange(Q_BLOCK)
        return diff_attention(qblk, ka, va, qpos, kpos, lam, t5_table)

    o = lax.map(one_block, (qblocks, jnp.arange(nblk)))
    return jnp.moveaxis(o, 0, 1).reshape(B, S, HA, 2 * DA)


def prompt_band(qb, kb, vb, rel_table):
    B, S = qb.shape[:2]
    nc = S // CHUNK
    pad = ((0, 0), (BAND_ROWS, 0), (0, 0), (0, 0))
    kp = jnp.pad(kb, pad)
    vp = jnp.pad(vb, pad)
    qchunks = jnp.moveaxis(qb.reshape(B, nc, CHUNK, HB, DB), 1, 0)

    def one_chunk(args):
        qch, c = args
        start = c * CHUNK
        kband = lax.dynamic_slice_in_dim(kp, start, BAND_ROWS + CHUNK, axis=1)
        vband = lax.dynamic_slice_in_dim(vp, start, BAND_ROWS + CHUNK, axis=1)
        qpos = start + jnp.arange(CHUNK)
        kpos = start - BAND_ROWS + jnp.arange(BAND_ROWS + CHUNK)
        return band_attention(qch, kband, vband, qpos, kpos, rel_table)

    o = lax.map(one_chunk, (qchunks, jnp.arange(nc)))
    return jnp.moveaxis(o, 0, 1).reshape(B, S, HB, DB)


def conv_ffn(h, conv_prev, w_up, w_gate, conv_w, conv_b, w_down):
    T = h.shape[1]
    u = jnp.einsum('btd,df->btf', h, w_up)
    g = jnp.einsum('btd,df->btf', h, w_gate)
    ext = jnp.concatenate([conv_prev.astype(u.dtype), u], axis=1)
    uc = conv_b + ext[:, 0:T] * conv_w[0]
    for j in range(1, CONV_W):
        uc = uc + ext[:, j:j + T] * conv_w[j]
    out = jnp.einsum('btf,fd->btd', jax.nn.gelu(uc) * g, w_down)
    return out, ext[:, -(CONV_W - 1):]


def block_tail(x, oa, ob, ga, gb, conv_prev, lam_init, subln_g, w_pa, w_pb, w_out,
               ln1_g, ln1_b, w_up, w_gate, conv_w, conv_b, w_down, ln2_g, ln2_b):
    B, T = x.shape[:2]
    ya = (head_rmsnorm(oa, subln_g) * (1.0 - lam_init)).reshape(B, T, WA) @ w_pa
    yb = ob.reshape(B, T, WB) @ w_pb
    mixed = (jax.nn.sigmoid(ga) * ya + jax.nn.sigmoid(gb) * yb) @ w_out
    h = layer_norm(ALPHA * x + mixed, ln1_g, ln1_b)
    f, conv_state = conv_ffn(h, conv_prev, w_up, w_gate, conv_w, conv_b, w_down)
    return layer_norm(ALPHA * h + f, ln2_g, ln2_b), conv_state


def setup_inputs(seed: int = 0) -> dict:
    key = jax.random.key(seed)
    ks = jax.random.split(key, 32)
    f32 = jnp.float32
    PB = min(BAND_ROWS, PAST_LEN)
    nrm = lambda k, s: jax.random.normal(k, s, f32)
    col_scale = jnp.concatenate([
        jnp.ones((2 * WA,), f32), jnp.full((WA,), BETA, f32),
        jnp.ones((2 * WB,), f32), jnp.full((WB,), BETA, f32),
        jnp.ones((2 * D_MODEL,), f32)])
    return {
        "x_prompt": nrm(ks[0], (BATCH, SEQ, D_MODEL)),
        "x_sample": nrm(ks[1], (DEC_BATCH, DEC_SEQ, D_MODEL)),
        "cache_a_k": nrm(ks[2], (DEPTH, DEC_BATCH, PAST_LEN, HA, 2, DA)),
        "cache_a_v": nrm(ks[3], (DEPTH, DEC_BATCH, PAST_LEN, HA, 2 * DA)) * BETA,
        "cache_b_k": nrm(ks[4], (DEPTH, DEC_BATCH, PB, HB, DB)),
        "cache_b_v": nrm(ks[5], (DEPTH, DEC_BATCH, PB, HB, DB)) * BETA,
        "cache_conv": nrm(ks[6], (DEPTH, DEC_BATCH, CONV_W - 1, D_FF)) * BETA,
        "t5_table": nrm(ks[7], (T5_BUCKETS, HA)) * 0.5,
        "w_in": nrm(ks[8], (DEPTH, D_MODEL, N_IN)) * (D_MODEL ** -0.5) * col_scale,
        "lambda_q1": nrm(ks[9], (DEPTH, DA)) * 0.1,
        "lambda_k1": nrm(ks[10], (DEPTH, DA)) * 0.1,
        "lambda_q2": nrm(ks[11], (DEPTH, DA)) * 0.1,
        "lambda_k2": nrm(ks[12], (DEPTH, DA)) * 0.1,
        "subln_g": 1.0 + 0.01 * nrm(ks[13], (DEPTH, 2 * DA)),
        "rel_table_b": nrm(ks[14], (DEPTH, 2 * MAX_REL + 1, HB)) * 0.5,
        "w_pa": nrm(ks[15], (DEPTH, WA, D_MODEL)) * (WA ** -0.5) * BETA,
        "w_pb": nrm(ks[16], (DEPTH, WB, D_MODEL)) * (WB ** -0.5) * BETA,
        "w_out": nrm(ks[17], (DEPTH, D_MODEL, D_MODEL)) * (D_MODEL ** -0.5) * BETA,
        "ln1_g": 1.0 + 0.01 * nrm(ks[18], (DEPTH, D_MODEL)),
        "ln1_b": 0.01 * nrm(ks[19], (DEPTH, D_MODEL)),
        "w_up": nrm(ks[20], (DEPTH, D_MODEL, D_FF)) * (D_MODEL ** -0.5) * BETA,
        "w_gate": nrm(ks[21], (DEPTH, D_MODEL, D_FF)) * (D_MODEL ** -0.5),
        "conv_w": nrm(ks[22], (DEPTH, CONV_W, D_FF)) * (CONV_W ** -0.5),
        "conv_b": 0.01 * nrm(ks[23], (DEPTH, D_FF)),
        "w_down": nrm(ks[24], (DEPTH, D_FF, D_MODEL)) * (D_FF ** -0.5) * BETA,
        "ln2_g": 1.0 + 0.01 * nrm(ks[25], (DEPTH, D_MODEL)),
        "ln2_b": 0.01 * nrm(ks[26], (DEPTH, D_MODEL)),
    }


def reference(x_prompt, x_sample, cache_a_k, cache_a_v, cache_b_k, cache_b_v, cache_conv,
              t5_table, w_in, lambda_q1, lambda_k1, lambda_q2, lambda_k2, subln_g, rel_table_b,
              w_pa, w_pb, w_out, ln1_g, ln1_b, w_up, w_gate, conv_w, conv_b, w_down, ln2_g, ln2_b):
    f32 = jnp.float32
    xp, xs = x_prompt, x_sample
    S = xp.shape[1]
    T = xs.shape[1]
    pa_k, pa_v, pb_k, pb_v, p_conv = [], [], [], [], []
    sa_k, sa_v, sb_k, sb_v, s_conv = [], [], [], [], []
    for l in range(DEPTH):
        lam_init = 0.8 - 0.6 * math.exp(-0.3 * l)
        lam = (jnp.exp(jnp.sum(lambda_q1[l].astype(f32) * lambda_k1[l].astype(f32)))
               - jnp.exp(jnp.sum(lambda_q2[l].astype(f32) * lambda_k2[l].astype(f32))) + lam_init)
        tail_w = (lam_init, subln_g[l], w_pa[l], w_pb[l], w_out[l], ln1_g[l], ln1_b[l],
                  w_up[l], w_gate[l], conv_w[l], conv_b[l], w_down[l], ln2_g[l], ln2_b[l])

        qa, ka, va, qb, kb, vb, ga, gb = split_proj(xp, w_in[l])
        oa = prompt_diff(qa, ka, va, lam, t5_table)
        ob = prompt_band(qb, kb, vb, rel_table_b[l])
        conv0 = jnp.zeros((xp.shape[0], CONV_W - 1, D_FF), xp.dtype)
        xp, conv_p = block_tail(xp, oa, ob, ga, gb, conv0, *tail_w)
        keep = min(BAND_ROWS, S)
        pa_k.append(ka)
        pa_v.append(va)
        pb_k.append(kb[:, S - keep:])
        pb_v.append(vb[:, S - keep:])
        p_conv.append(conv_p)

        P = cache_a_k.shape[2]
        PB = cache_b_k.shape[2]
        qa, ka, va, qb, kb, vb, ga, gb = split_proj(xs, w_in[l])
        qpos = P + jnp.arange(T)
        ka_all = jnp.concatenate([cache_a_k[l].astype(ka.dtype), ka], axis=1)
        va_all = jnp.concatenate([cache_a_v[l].astype(va.dtype), va], axis=1)
        oa = diff_attention(qa, ka_all, va_all, qpos, jnp.arange(P + T), lam, t5_table)
        kb_all = jnp.concatenate([cache_b_k[l].astype(kb.dtype), kb], axis=1)
        vb_all = jnp.concatenate([cache_b_v[l].astype(vb.dtype), vb], axis=1)
        kpos_b = jnp.concatenate([jnp.arange(P - PB, P), P + jnp.arange(T)])
        ob = band_attention(qb, kb_all, vb_all, qpos, kpos_b, rel_table_b[l])
        xs, conv_s = block_tail(xs, oa, ob, ga, gb, cache_conv[l], *tail_w)
        sa_k.append(ka)
        sa_v.append(va)
        sb_k.append(kb)
        sb_v.append(vb)
        s_conv.append(conv_s)

    return (xp, xs,
            jnp.stack(pa_k), jnp.stack(pa_v), jnp.stack(pb_k), jnp.stack(pb_v), jnp.stack(p_conv),
            jnp.stack(sa_k), jnp.stack(sa_v), jnp.stack(sb_k), jnp.stack(sb_v), jnp.stack(s_conv))
```

```python
import math
import contextlib
import numpy as np
import concourse.bass as bass
import concourse.mybir as mybir
from concourse.bass_utils import run_bass_kernel_spmd

F32 = mybir.dt.float32
BF16 = mybir.dt.bfloat16
AF = mybir.ActivationFunctionType
ALU = mybir.AluOpType
AX = mybir.AxisListType

N_CORES = 8
D = 1024
S = 4096
NBLK = 8
TS = 64
DFF = 2816
NFC = 22
ALPHA = 2.0 ** 0.25
LAM_INIT = 0.2
EPS = 1e-5
NTOK = S + 2 * TS

COMPUTE = ("pe", "act", "dve", "pool")
N_DMA_SLOTS = {"sp": 12, "pool": 6}


def ap_box(ap):
    t = ap.tensor
    name = t.name
    pat = [list(x) for x in ap.ap]
    off = int(ap.offset)
    if "DRam" in type(t).__name__:
        lo = hi = off
        for st, n in pat:
            if n > 1:
                if st >= 0:
                    hi += st * (n - 1)
                else:
                    lo += st * (n - 1)
        return (name, 0, 0, lo, hi)
    fsz = 1
    for s_ in list(t.shape)[1:]:
        fsz *= s_
    pstep, pn = pat[0]
    p0 = off // fsz
    foff = off - p0 * fsz
    ps = pstep // fsz
    lo = hi = foff
    for st, n in pat[1:]:
        if n > 1:
            if st >= 0:
                hi += st * (n - 1)
            else:
                lo += st * (n - 1)
    return (name, p0, p0 + (pn - 1) * ps, lo, hi)


def boxes_overlap(a, b):
    return not (a[2] < b[1] or b[2] < a[1] or a[4] < b[3] or b[4] < a[3])


def box_contains(a, b):
    return a[1] <= b[1] and a[2] >= b[2] and a[3] <= b[3] and a[4] >= b[4]


class Op:
    __slots__ = ("eng", "fn", "reads", "writes", "dma", "idx", "deps", "sig", "slot", "slotval", "signals")

    def __init__(self, eng, fn, reads, writes, dma):
        self.eng = eng
        self.fn = fn
        self.reads = reads
        self.writes = writes
        self.dma = dma
        self.deps = set()
        self.sig = None
        self.slot = None
        self.slotval = None
        self.signals = False


class Prog:
    def __init__(self, nc):
        self.nc = nc
        self.ops = []
        self.hist = {}

    def op(self, eng, fn, reads=(), writes=(), dma=False):
        rb, wb = [], []
        for a in reads:
            if "PSum" in type(a.tensor).__name__:
                wb.append((a.tensor.name, 0, 127, 0, 1 << 30))
            else:
                rb.append(ap_box(a))
        for a in writes:
            if "PSum" in type(a.tensor).__name__:
                wb.append((a.tensor.name, 0, 127, 0, 1 << 30))
            else:
                wb.append(ap_box(a))
        o = Op(eng, fn, rb, wb, dma)
        o.idx = len(self.ops)
        self.ops.append(o)
        ops = self.ops
        for b in o.reads:
            for (hb, hi, hw) in self.hist.setdefault(b[0], []):
                if hw and boxes_overlap(hb, b):
                    o.deps.add(hi)
        for b in o.writes:
            for (hb, hi, hw) in self.hist.setdefault(b[0], []):
                if boxes_overlap(hb, b):
                    o.deps.add(hi)
        o.deps.discard(o.idx)
        for b in o.writes:
            h = self.hist[b[0]]
            h[:] = [e for e in h if not box_contains(b, e[0])]
            h.append((b, o.idx, True))
        for b in o.reads:
            h = self.hist[b[0]]
            if not o.dma:
                h[:] = [e for e in h if not ((not e[2]) and e[0] == b and ops[e[1]].eng == o.eng
                                             and not ops[e[1]].dma)]
            h.append((b, o.idx, False))
        return o

    def emit(self):
        nc = self.nc
        ops = self.ops
        for o in ops:
            for d in list(o.deps):
                p = ops[d]
                if (not p.dma) and (not o.dma) and p.eng == o.eng and p.eng == "pe":
                    o.deps.discard(d)
                    continue
                p.signals = True
        cnt = {e: 0 for e in COMPUTE}
        slot_rr = {q: 0 for q in N_DMA_SLOTS}
        slot_cnt = {q: [0] * n for q, n in N_DMA_SLOTS.items()}
        for o in ops:
            if o.dma:
                q = o.eng
                s = slot_rr[q]
                slot_rr[q] = (s + 1) % N_DMA_SLOTS[q]
                slot_cnt[q][s] += 16
                o.slot = (q, s)
                o.slotval = slot_cnt[q][s]
            elif o.signals:
                cnt[o.eng] += 1
                o.sig = cnt[o.eng]
        with contextlib.ExitStack() as es:
            sems = {e: es.enter_context(nc.semaphore("s_" + e)) for e in COMPUTE}
            dsems = {q: [es.enter_context(nc.semaphore("d_%s%d" % (q, i))) for i in range(n)]
                     for q, n in N_DMA_SLOTS.items()}
            block = es.enter_context(nc.Block())
            by_eng = {}
            for o in ops:
                by_eng.setdefault(o.eng, []).append(o)
            self.n_waits = 0

            def run_engine(ename, eng):
                waited = {}

                def wait(key, sem, val):
                    if waited.get(key, 0) >= val:
                        return
                    eng.wait_ge(sem, val)
                    self.n_waits += 1
                    waited[key] = val

                for o in by_eng.get(ename, []):
                    need = {}
                    for d in o.deps:
                        p = ops[d]
                        if p.dma:
                            k = ("d",) + p.slot
                            need[k] = max(need.get(k, 0), p.slotval)
                        else:
                            k = ("c", p.eng)
                            need[k] = max(need.get(k, 0), p.sig)
                    if o.dma:
                        k = ("d",) + o.slot
                        need[k] = max(need.get(k, 0), o.slotval - 16)
                    for k, v in sorted(need.items()):
                        if v <= 0:
                            continue
                        if k[0] == "c":
                            wait(k, sems[k[1]], v)
                        else:
                            wait(k, dsems[k[1]][k[2]], v)
                    ins = o.fn(eng)
                    if o.dma:
                        ins.then_inc(dsems[o.slot[0]][o.slot[1]], 16)
                    elif o.signals:
                        ins.then_inc(sems[o.eng], 1)
                if ename == "sp":
                    for q, n in N_DMA_SLOTS.items():
                        for s in range(n):
                            if slot_cnt[q][s] > 0:
                                wait(("d", q, s), dsems[q][s], slot_cnt[q][s])

            @block.tensor
            def _(e):
                run_engine("pe", e)

            @block.scalar
            def _(e):
                run_engine("act", e)

            @block.vector
            def _(e):
                run_engine("dve", e)

            @block.gpsimd
            def _(e):
                run_engine("pool", e)

            @block.sync
            def _(e):
                run_engine("sp", e)

    def dma(self, q, out, in_):
        return self.op(q, lambda e: e.dma_start(out=out, in_=in_), reads=[in_], writes=[out], dma=True)

    def mm(self, out, lhsT, rhs, start=True, stop=True):
        return self.op("pe", lambda e: e.matmul(out, lhsT, rhs, start=start, stop=stop),
                       reads=[lhsT, rhs], writes=[out])

    def transpose(self, out, in_, ident):
        return self.op("pe", lambda e: e.transpose(out, in_, ident), reads=[in_, ident], writes=[out])

    def act(self, out, in_, func, bias=None, scale=1.0):
        kw = {}
        rd = [in_]
        if bias is not None:
            kw["bias"] = bias
            if not isinstance(bias, (int, float)):
                rd.append(bias)
        return self.op("act", lambda e: e.activation(out, in_, func, scale=scale, **kw), reads=rd, writes=[out])

    def tt(self, eng, out, in0, in1, op):
        return self.op(eng, lambda e: e.tensor_tensor(out, in0, in1, op), reads=[in0, in1], writes=[out])

    def ts(self, eng, out, in0, s1, s2, op0, op1=None):
        rd = [in0] + [s for s in (s1, s2) if s is not None and not isinstance(s, (int, float))]
        if op1 is None:
            return self.op(eng, lambda e: e.tensor_scalar(out, in0, s1, s2, op0), reads=rd, writes=[out])
        return self.op(eng, lambda e: e.tensor_scalar(out, in0, s1, s2, op0, op1), reads=rd, writes=[out])

    def stt(self, eng, out, in0, scalar, in1, op0, op1):
        rd = [in0, in1] + ([] if isinstance(scalar, (int, float)) else [scalar])
        return self.op(eng, lambda e: e.scalar_tensor_tensor(out, in0, scalar, in1, op0, op1), reads=rd, writes=[out])

    def copy(self, eng, out, in_):
        if eng == "act":
            return self.op(eng, lambda e: e.copy(out, in_), reads=[in_], writes=[out])
        return self.op(eng, lambda e: e.tensor_copy(out, in_), reads=[in_], writes=[out])

    def memset(self, eng, out, val):
        return self.op(eng, lambda e: e.memset(out, val), reads=[], writes=[out])

    def recip(self, out, in_):
        return self.op("dve", lambda e: e.reciprocal(out, in_), reads=[in_], writes=[out])

    def recip_fast(self, out, in_):
        return self.op("dve", lambda e: e.reciprocal_approx_fast(out, in_), reads=[in_], writes=[out])


def carve(T, off, shape):
    n = 1
    for s_ in shape[1:]:
        n *= s_
    ap = T[:, off:off + n]
    if len(shape) == 3:
        ap = ap.rearrange("p (a b) -> p a b", a=shape[1])
    return ap


def v3(ap, a):
    return ap.rearrange("p (a b) -> p a b", a=a)


def build_nc(n_units=16, n_cblocks=9):
    nc = bass.Bass("TRN2", target_bir_lowering=False)

    def di(n, s):
        return nc.dram_tensor(n, s, F32, kind="ExternalInput")

    def do(n, s):
        return nc.dram_tensor(n, s, F32, kind="ExternalOutput")

    xp = di("xp", [S, D])
    xs = di("xs", [2 * TS, D])
    cak = di("cak", [2, 2048, D])
    cav = di("cav", [2, 2048, D])
    cbk = di("cbk", [2, 512, D])
    cbv = di("cbv", [2, 512, D])
    cconv = di("cconv", [2, 2, DFF])
    w_in = di("w_in", [D, 8192])
    w_pa = di("w_pa", [D, D])
    w_pb = di("w_pb", [D, D])
    w_out = di("w_out", [D, D])
    w_up = di("w_up", [D, DFF])
    w_gate = di("w_gate", [D, DFF])
    w_down = di("w_down", [DFF, D])
    lamv = di("lamv", [128, 4, 64])
    sublng = di("sublng", [128, 1])
    lntab = di("lntab", [128, 4, D])
    convp = di("convp", [128, NFC, 4])
    gtab = di("gtab", [128, 24, 256])
    ctab_d = di("ctab", [128, 24])
    idn = di("idn", [128, 128])

    y_p = do("y_p", [S, D])
    y_s = do("y_s", [2 * TS, D])
    oak_p = do("oak_p", [S, D])
    oav_p = do("oav_p", [S, D])
    obk_p = do("obk_p", [512, D])
    obv_p = do("obv_p", [512, D])
    oconv_p = do("oconv_p", [2, DFF])
    oak_s = do("oak_s", [2 * TS, D])
    oav_s = do("oav_s", [2 * TS, D])
    obk_s = do("obk_s", [2 * TS, D])
    obv_s = do("obv_s", [2 * TS, D])
    oconv_s = do("oconv_s", [4, DFF])

    ond = nc.dram_tensor("ond", [16, 128, NTOK], BF16)
    wg_b = nc.dram_tensor("wg_b", [D, 2048], BF16)
    wpa_b = nc.dram_tensor("wpa_b", [D, D], BF16)
    wpb_b = nc.dram_tensor("wpb_b", [D, D], BF16)
    wo_b = nc.dram_tensor("wo_b", [D, D], BF16)
    wup_b = nc.dram_tensor("wup_b", [D, DFF], BF16)
    wgt_b = nc.dram_tensor("wgt_b", [D, DFF], BF16)
    wdn_b = nc.dram_tensor("wdn_b", [DFF, D], BF16)

    with contextlib.ExitStack() as es:
        def sb(n, s, d):
            return es.enter_context(nc.sbuf_tensor(n, s, d))

        AR = sb("AR", [128, 65536], BF16)
        FA = sb("FA", [128, 14336], F32)
        ET = [sb("ET%d" % i, [128, 512], BF16) for i in range(6)]
        ident = sb("ident", [128, 128], F32)
        onesb = sb("onesb", [128, 128], BF16)
        onesf = sb("onesf", [128, 128], F32)
        epsb = sb("epsb", [128, 1], F32)
        lamt = sb("lamt", [128, 8], F32)
        ctab = sb("ctab_s", [128, 24], F32)
        lamv_s = sb("lamv_s", [128, 4, 64], F32)
        convp_s = sb("convp_s", [128, NFC, 4], F32)
        hal = sb("hal", [128, NFC, 2], F32)
        cch = [sb("cch%d" % s, [128, NFC, 2], F32) for s in range(2)]
        bst = sb("bst", [128, 4, 12], F32)
        mv = sb("mv", [128, 4, 2], F32)
        rstd = sb("rstd", [128, 4], F32)
        BK = [es.enter_context(nc.psum_tensor("BK%d" % i, [128, 512], F32)) for i in range(8)]

        P = Prog(nc)
        rows = [slice(0, 64), slice(64, 128)]

        xT = carve(AR, 0, [128, 8, NTOK])
        QT = carve(AR, 33792, [128, NTOK])
        KT = carve(AR, 38016, [128, NTOK])
        Vt = carve(AR, 42240, [128, 32, 128])
        Vs = carve(AR, 46336, [128, 2, 128])
        OnT = carve(AR, 46592, [128, NTOK])
        Wh = [carve(AR, 50816, [128, 8, 384]), carve(AR, 53888, [128, 8, 384])]
        ckT = [carve(AR, 56960, [128, 2048]), carve(AR, 59008, [128, 2048])]
        cV = [carve(AR, 61056, [128, 16, 128]), carve(AR, 63104, [128, 16, 128])]
        gt = [carve(FA, 0, [128, 2, 256]), carve(FA, 512, [128, 2, 256])]
        kvst = [carve(FA, 1024, [128, 4, 256]), carve(FA, 2048, [128, 4, 256])]
        tmpn = [carve(FA, 3072, [128, 512]), carve(FA, 3584, [128, 512]), carve(FA, 10304, [128, 512])]
        fb = [carve(FA, 4096 + 512 * i, [128, 512]) for i in range(10)]
        ckf = [carve(FA, 9216, [128, 4, 128]), carve(FA, 9728, [128, 4, 128])]
        lprod = carve(FA, 10240, [128, 64])
        fbo = [carve(FA, 10816, [128, 512]), carve(FA, 11328, [128, 512])]
        accb = [[carve(FA, 11840, [128, 512]), carve(FA, 12352, [128, 512])],
                [carve(FA, 12864, [128, 512]), carve(FA, 13376, [128, 512])]]

        STB = [BK[0], BK[1], BK[6]]
        OTB = [BK[2], BK[3]]
        SMB = [BK[4], BK[5]]
        PJ = [BK[7], BK[0], BK[1], BK[6]]
        cnt = {"pj": 0, "item": 0, "kv": 0, "ckf": 0, "nb": 0, "fbo": 0, "acc": 0}

        def pj():
            cnt["pj"] += 1
            return PJ[cnt["pj"] % 4]

        P.dma("sp", ident[:], idn.ap())
        P.dma("sp", lamv_s[:], lamv.ap())
        P.dma("sp", ctab[:], ctab_d.ap())
        P.dma("sp", convp_s[:], convp.ap())
        P.dma("sp", lamt[:, 5:6], sublng.ap())
        P.memset("dve", onesb[:], 1.0)
        P.memset("dve", onesf[:], 1.0)
        P.memset("dve", epsb[:], EPS)
        P.memset("dve", hal[:], 0.0)
        for i in range(2):
            P.tt("dve", lprod, lamv_s[:, 2 * i, :], lamv_s[:, 2 * i + 1, :], ALU.mult)
            P.op("dve", lambda e, i=i: e.reduce_sum(lamt[:, i:i + 1], lprod, AX.X), reads=[lprod],
                 writes=[lamt[:, i:i + 1]])
        P.act(lamt[:, 2:4], lamt[:, 0:2], AF.Exp)
        P.tt("dve", lamt[:, 4:5], lamt[:, 3:4], lamt[:, 2:3], ALU.subtract)
        P.ts("dve", lamt[:, 4:5], lamt[:, 4:5], -LAM_INIT, None, ALU.add)
        P.ts("dve", lamt[:, 5:6], lamt[:, 5:6], 1.0 - LAM_INIT, None, ALU.mult)
        neglam = lamt[:, 4:5]
        gsub = lamt[:, 5:6]

        def load_unit_weights(u):
            isA = u < 8
            qc0 = (0 if isA else 3072) + (u % 8) * 128
            W = Wh[u % 2]
            for i, c0 in enumerate((qc0, qc0 + 1024, qc0 + 2048)):
                P.dma("pool", W[:, :, i * 128:(i + 1) * 128],
                      w_in[:, c0:c0 + 128].rearrange("(kc p) n -> p kc n", p=128))
            if isA:
                P.dma("sp", gt[u % 2][:, 0:1, :], gtab[:, u:u + 1, :])
            else:
                j = u - 8
                P.dma("sp", gt[u % 2][:, 0:2, :], gtab[:, 8 + 2 * j:10 + 2 * j, :])

        load_unit_weights(0)

        wcasts = []
        for r0 in range(0, D, 256):
            wcasts.append((wg_b[r0:r0 + 256, :], w_in[r0:r0 + 256, 6144:8192]))
            wcasts.append((wpa_b[r0:r0 + 256, :], w_pa[r0:r0 + 256, :]))
            wcasts.append((wpb_b[r0:r0 + 256, :], w_pb[r0:r0 + 256, :]))
            wcasts.append((wo_b[r0:r0 + 256, :], w_out[r0:r0 + 256, :]))
            wcasts.append((wup_b[r0:r0 + 256, :], w_up[r0:r0 + 256, :]))
            wcasts.append((wgt_b[r0:r0 + 256, :], w_gate[r0:r0 + 256, :]))
        for r0 in range(0, DFF, 256):
            wcasts.append((wdn_b[r0:r0 + 256, :], w_down[r0:r0 + 256, :]))
        wc_i = [0]

        def emit_wcasts(k):
            for _ in range(k):
                if wc_i[0] < len(wcasts):
                    d_, s_ = wcasts[wc_i[0]]
                    P.dma("pool", d_, s_)
                    wc_i[0] += 1

        def transpose_tile(src, dstT, col0, width=128):
            for g in range(2):
                bank = BK[cnt["nb"] % 8]
                cnt["nb"] += 1
                for i in range(4):
                    kc = g * 4 + i
                    P.transpose(bank[:, i * 128:(i + 1) * 128], src[:, kc * 128:(kc + 1) * 128], ident[:])
                P.copy("act" if g == 0 else "dve", dstT[:, 4 * g:4 * g + 4, col0:col0 + 128], v3(bank[:], 4))

        for t in range(33):
            stage = kvst[t % 2].rearrange("p a b -> p (a b)")
            src = xp[t * 128:(t + 1) * 128, :] if t < 32 else xs.ap()
            P.dma("sp", stage, src)
            transpose_tile(stage, xT, t * 128)

        def attn(tiles, qcol0, tsel, csel, accp):
            items = [(ti, m) for ti in range(len(tiles)) for m in range(2)]
            pend = []
            LOOK = 2
            DEFER_AT = 8
            n_done = [0]

            def flush_one():
                T, m, et, ti = pend.pop(0)
                nk, a, b = T["nk"], T["c_lo"], T["c_hi"]
                first = ti == 0
                last = ti == len(tiles) - 1
                P.mm(OTB[m][:, a:b], T["V"], et[0:nk, a:b], start=first, stop=last)

            for (ti, m) in items:
                T = tiles[ti]
                k = cnt["item"]
                cnt["item"] += 1
                st = STB[k % 3]
                et = ET[k % 6]
                tm = tmpn[k % 3]
                nk, a, b = T["nk"], T["c_lo"], T["c_hi"]
                qa = qcol0 + (a - T["cbase"])
                P.mm(st[0:nk, a:b], T["kT"](m), QT[rows[m], qa:qa + (b - a)])
                for (pa, pb, kind, tc0) in T["parts"]:
                    if kind == "far":
                        P.act(et[0:nk, pa:pb], st[0:nk, pa:pb], AF.Exp,
                              bias=ctab[0:nk, csel(m):csel(m) + 1], scale=0.125)
                    else:
                        P.stt("dve", tm[0:nk, pa:pb], st[0:nk, pa:pb], 0.125,
                              T["gt"][0:nk, tsel(m), tc0:tc0 + (pb - pa)], ALU.mult, ALU.add)
                        P.act(et[0:nk, pa:pb], tm[0:nk, pa:pb], AF.Exp)
                if T["corner"] is not None:
                    r0, r1, c0, c1 = T["corner"]
                    P.memset("dve", et[r0:r1, c0:c1], 0.0)
                if ti == 0:
                    P.copy("dve", accp[m][0:nk, a:b], et[0:nk, a:b])
                else:
                    P.tt("dve", accp[m][0:nk, a:b], accp[m][0:nk, a:b], et[0:nk, a:b], ALU.add)
                pend.append((T, m, et, ti))
                if len(pend) > LOOK:
                    flush_one()
                if n_done[0] == DEFER_AT:
                    run_deferred()
                n_done[0] += 1
            while pend:
                flush_one()
            run_deferred()

        deferred = []

        def run_deferred():
            while deferred:
                deferred.pop(0)()

        def finalize(isA, c0, n, dst, accp):
            cs = slice(c0, c0 + n)
            for m in range(2):
                P.mm(SMB[m][:, cs], onesf[:], accp[m][:, cs])
            if isA:
                fo_ = fbo[cnt["fbo"] % 2]
                cnt["fbo"] += 1
                P.copy("act", fb[6][:, :n], SMB[0][:, cs])
                P.copy("dve", fb[7][:, :n], OTB[0][:, cs])
                P.copy("act", fb[8][:, :n], SMB[1][:, cs])
                P.copy("dve", fb[9][:, :n], OTB[1][:, cs])
                P.recip(fb[0][:, :n], fb[6][:, :n])
                P.tt("dve", fb[1][:, :n], fb[7][:, :n], fb[0][:, :n], ALU.mult)
                P.recip(fb[2][:, :n], fb[8][:, :n])
                P.tt("dve", fb[3][:, :n], fb[9][:, :n], fb[2][:, :n], ALU.mult)
                P.stt("dve", fo_[:, :n], fb[3][:, :n], neglam, fb[1][:, :n], ALU.mult, ALU.add)
                P.tt("dve", fb[5][:, :n], fo_[:, :n], fo_[:, :n], ALU.mult)

                def tail(fo_=fo_, n=n, dst=dst):
                    bank = BK[7]
                    P.mm(bank[:, :n], onesf[:], fb[5][:, :n])
                    P.act(fb[4][:, :n], bank[:, :n], AF.Sqrt, bias=epsb[:], scale=1.0 / 128)
                    P.recip(fb[4][:, :n], fb[4][:, :n])
                    P.stt("dve", dst, fo_[:, :n], gsub, fb[4][:, :n], ALU.mult, ALU.mult)
                deferred.append(tail)
            else:
                for m in range(2):
                    r = rows[m]
                    P.copy("act", fb[6][r, :n], SMB[m][r, cs])
                    P.copy("dve", fb[7][r, :n], OTB[m][r, cs])
                P.recip(fb[0][:, :n], fb[6][:, :n])
                P.tt("dve", dst, fb[7][:, :n], fb[0][:, :n], ALU.mult)

        def prompt_tiles(isA, qb, g):
            def mk(kt, c_lo, c_hi, parts, corner):
                return dict(kT=lambda m, kt=kt: KT[rows[m], kt * 128:(kt + 1) * 128], V=Vt[:, kt, :], nk=128,
                            c_lo=c_lo, c_hi=c_hi, cbase=0, parts=parts, corner=corner, gt=g)

            def diag(i):
                c0 = 128 * i
                parts = [(c0, min(c0 + 256, 512), "near", 0)]
                if c0 + 256 < 512:
                    parts.append((c0 + 256, 512, "far", 0))
                return mk(4 * qb + i, c0, 512, parts, (64, 128, c0, c0 + 64))

            tiles = []
            if isA:
                for kt in range(0, 4 * qb - 1):
                    tiles.append(mk(kt, 0, 512, [(0, 512, "far", 0)], None))
                if qb > 0:
                    tiles.append(mk(4 * qb - 1, 0, 512, [(0, 128, "near", 128), (128, 512, "far", 0)], None))
                for i in range(4):
                    tiles.append(diag(i))
            else:
                tiles.append(diag(0))
                if qb > 0:
                    for ip in range(4):
                        c_hi = 128 * (ip + 1)
                        corner = (0, 64, 128 * ip + 64, 128 * ip + 128)
                        if ip < 3:
                            parts = [(0, c_hi, "far", 0)]
                        else:
                            parts = [(0, 128, "near", 128), (128, 512, "far", 0)]
                        tiles.append(mk(4 * qb - 4 + ip, 0, c_hi, parts, corner))
                for i in range(1, 4):
                    tiles.append(diag(i))
            return tiles

        def sample_tiles(isA, s, g):
            ncache = 16 if isA else 4
            cb = 64 * s
            tiles = []
            for t in range(ncache):
                parts = [(cb, cb + 64, "far", 0)] if t < ncache - 1 else [(cb, cb + 64, "near", 128)]
                tiles.append(dict(kT=lambda m, t=t: ckT[s][rows[m], t * 128:(t + 1) * 128], V=cV[s][:, t, :],
                                  nk=128, c_lo=cb, c_hi=cb + 64, cbase=cb, parts=parts, corner=None, gt=g))
            tiles.append(dict(kT=lambda m: KT[rows[m], S + 64 * s:S + 64 * s + 64], V=Vs[0:64, s, :], nk=64,
                              c_lo=cb, c_hi=cb + 64, cbase=cb, parts=[(cb, cb + 64, "near", 0)], corner=None,
                              gt=g))
            return tiles

        n_wc_per_unit = (len(wcasts) + 15) // 16
        for u in range(n_units):
            isA = u < 8
            hj = u % 8
            W = Wh[u % 2]
            g = gt[u % 2]
            if u + 1 < n_units:
                load_unit_weights(u + 1)
            emit_wcasts(n_wc_per_unit)
            colsK = slice(hj * 128, hj * 128 + 128)
            if isA:
                tsel = lambda m: 0
                csel = lambda m, hj=hj: hj
                okd, ovd, oks, ovs = oak_p, oav_p, oak_s, oav_s
                ck_src, cv_src, ncch = cak, cav, 4
            else:
                tsel = lambda m: m
                csel = lambda m, hj=hj: 8 + 2 * hj + m
                okd, ovd, oks, ovs = obk_p, obv_p, obk_s, obv_s
                ck_src, cv_src, ncch = cbk, cbv, 1
            for s in range(2):
                P.dma("pool", cV[s][:, 0:4 * ncch, :],
                      cv_src[s, :, colsK].rearrange("(t p) c -> p t c", p=128))
            for blk in range(NBLK):
                cols = slice(blk * 512, blk * 512 + 512)
                for wi, dst in ((0, QT), (1, KT)):
                    bank = pj()
                    for kc in range(8):
                        P.mm(bank[:, :], W[:, kc, wi * 128:(wi + 1) * 128], xT[:, kc, cols],
                             start=(kc == 0), stop=(kc == 7))
                    P.copy("act", dst[:, cols], bank[:])
                need_out = isA or blk == NBLK - 1
                if need_out:
                    st_ = kvst[cnt["kv"] % 2]
                    cnt["kv"] += 1
                    for half in range(2):
                        bank = pj()
                        for tt in range(2):
                            t = blk * 4 + half * 2 + tt
                            for kc in range(8):
                                P.mm(bank[:, tt * 256:(tt + 1) * 256], xT[:, kc, t * 128:(t + 1) * 128],
                                     W[:, kc, 128:384], start=(kc == 0), stop=(kc == 7))
                        P.copy("act", st_[:, half * 2:half * 2 + 2, :], v3(bank[:], 2))
                        P.copy("dve", Vt[:, blk * 4 + half * 2:blk * 4 + half * 2 + 2, :],
                               st_[:, half * 2:half * 2 + 2, 128:256])
                    r0 = blk * 512 if isA else 0
                    P.dma("sp", okd[r0:r0 + 512, colsK].rearrange("(t p) c -> p t c", p=128), st_[:, :, 0:128])
                    P.dma("sp", ovd[r0:r0 + 512, colsK].rearrange("(t p) c -> p t c", p=128), st_[:, :, 128:256])
                else:
                    bank = pj()
                    for tt in range(4):
                        t = blk * 4 + tt
                        for kc in range(8):
                            P.mm(bank[:, tt * 128:(tt + 1) * 128], xT[:, kc, t * 128:(t + 1) * 128],
                                 W[:, kc, 256:384], start=(kc == 0), stop=(kc == 7))
                    P.copy("dve", Vt[:, blk * 4:blk * 4 + 4, :], v3(bank[:], 4))
                accp = accb[cnt["acc"] % 2]
                cnt["acc"] += 1
                attn(prompt_tiles(isA, blk, g), blk * 512, tsel, csel, accp)
                finalize(isA, 0, 512, OnT[:, cols], accp)
            cs_ = slice(S, S + 128)
            for wi, dst in ((0, QT), (1, KT)):
                bank = pj()
                for kc in range(8):
                    P.mm(bank[:, 0:128], W[:, kc, wi * 128:(wi + 1) * 128], xT[:, kc, cs_],
                         start=(kc == 0), stop=(kc == 7))
                P.copy("dve", dst[:, cs_], bank[:, 0:128])
            st_ = kvst[cnt["kv"] % 2]
            cnt["kv"] += 1
            bank = pj()
            for s in range(2):
                for kc in range(8):
                    P.mm(bank[0:64, s * 256:(s + 1) * 256], xT[:, kc, S + 64 * s:S + 64 * s + 64],
                         W[:, kc, 128:384], start=(kc == 0), stop=(kc == 7))
            P.copy("dve", st_[0:64, 0:2, :], v3(bank[0:64, :], 2))
            P.copy("dve", Vs[0:64, :, :], st_[0:64, 0:2, 128:256])
            P.dma("sp", oks[:, colsK].rearrange("(s t) c -> t s c", s=2), st_[0:64, 0:2, 0:128])
            P.dma("sp", ovs[:, colsK].rearrange("(s t) c -> t s c", s=2), st_[0:64, 0:2, 128:256])
            accp = accb[cnt["acc"] % 2]
            cnt["acc"] += 1
            for s in range(2):
                for c4 in range(ncch):
                    cf = ckf[cnt["ckf"] % 2]
                    cnt["ckf"] += 1
                    P.dma("sp", cf[:], ck_src[s, c4 * 512:(c4 + 1) * 512, colsK].rearrange("(t p) c -> p t c", p=128))
                    bank = pj()
                    for i in range(4):
                        P.transpose(bank[:, i * 128:(i + 1) * 128], cf[:, i, :], ident[:])
                    P.copy("dve", ckT[s][:, c4 * 512:(c4 + 1) * 512], bank[:])
                attn(sample_tiles(isA, s, g), S + 64 * s, tsel, csel, accp)
            finalize(isA, 0, 128, OnT[:, cs_], accp)
            deferred.append(lambda u=u: P.dma("sp", ond[u], OnT))
        run_deferred()
        emit_wcasts(len(wcasts))

        NR = 7
        HELD = 4
        ring = [carve(AR, i * 4096, [128, 8, 512]) for i in range(NR)]
        zT = carve(AR, 28672, [128, NFC, 512])
        mT = carve(AR, 39936, [128, 8, 512])
        hT = carve(AR, 44032, [128, 8, 512])
        xTb = carve(AR, 48128, [128, 8, 512])
        onb = carve(AR, 52224, [128, 8, 512])
        obb = carve(AR, 56320, [128, 8, 512])
        xres = carve(FA, 0, [128, 4, D])
        lnt = carve(FA, 4096, [128, 4, D])
        sgA = carve(FA, 8192, [128, 512])
        sgB = carve(FA, 8704, [128, 512])
        m1 = carve(FA, 9216, [128, 512])
        ubuf = carve(FA, 9728, [128, 520])
        cva = carve(FA, 10248, [128, 512])
        cvg = carve(FA, 10760, [128, 512])
        utok = carve(FA, 11272, [128, DFF])

        def nb():
            b_ = BK[cnt["nb"] % 8]
            cnt["nb"] += 1
            return b_

        chunks = []
        for b in range(n_cblocks):
            for hf in range(2):
                c = slice(hf * 512, hf * 512 + 512)
                c2 = slice(1024 + hf * 512, 1024 + hf * 512 + 512)
                chunks += [(wg_b[:, c], 8, 512), (wpa_b[:, c], 8, 512), (wg_b[:, c2], 8, 512), (wpb_b[:, c], 8, 512)]
            chunks += [(wo_b[:, 0:512], 8, 512), (wo_b[:, 512:1024], 8, 512)]
            for ci in range(6):
                w = 512 if ci < 5 else 256
                chunks += [(wup_b[:, ci * 512:ci * 512 + w], 8, w), (wgt_b[:, ci * 512:ci * 512 + w], 8, w)]
            for cg in range(2):
                for kg in range(3):
                    nkc = 8 if kg < 2 else 6
                    chunks.append((wdn_b[kg * 1024:kg * 1024 + nkc * 128, cg * 512:(cg + 1) * 512], nkc, 512))
        wstate = {"issued": 0, "got": 0}

        def w_issue():
            k = wstate["issued"]
            if k >= len(chunks):
                return
            src, nkc, w = chunks[k]
            P.dma("sp", ring[k % NR][:, 0:nkc, 0:w], src.rearrange("(kc p) n -> p kc n", p=128))
            wstate["issued"] += 1

        def w_get():
            k = wstate["got"]
            wstate["got"] += 1
            while wstate["issued"] < min(len(chunks), k + NR - HELD + 1):
                w_issue()
            return ring[k % NR]

        if n_cblocks > 0:
            P.dma("sp", lnt, lntab.ap())
            ccs = FA[0:2, 11272:11272 + DFF]
            for s in range(2):
                P.dma("sp", ccs, cconv[s])
                bank = nb()
                for fo in range(NFC):
                    P.mm(bank[:, 2 * fo:2 * fo + 2], ccs[:, fo * 128:(fo + 1) * 128], ident[0:2, 0:2])
                P.copy("dve", cch[s][:], v3(bank[:, 0:2 * NFC], NFC))

        def ln_stats(xt, t):
            for c in range(2):
                P.op("dve", lambda e, c=c: e.bn_stats(bst[:, t, 6 * c:6 * c + 6], xt[:, c * 512:(c + 1) * 512]),
                     reads=[xt[:, c * 512:(c + 1) * 512]], writes=[bst[:, t, 6 * c:6 * c + 6]])
            P.op("dve", lambda e: e.bn_aggr(mv[:, t, :], bst[:, t, :]), reads=[bst[:, t, :]], writes=[mv[:, t, :]])

        def ln_rstd(nt):
            P.act(rstd[:, 0:nt], mv[:, 0:nt, 1], AF.Sqrt, bias=epsb[:], scale=1.0)
            P.recip(rstd[:, 0:nt], rstd[:, 0:nt])

        def ln_apply(xt, gi, t):
            P.ts("dve", xt, xt, mv[:, t, 0:1], rstd[:, t:t + 1], ALU.subtract, ALU.mult)
            P.tt("dve", xt, xt, lnt[:, gi, :], ALU.mult)
            P.tt("pool", xt, xt, lnt[:, gi + 1, :], ALU.add)

        stg = [carve(FA, 11272, [128, D]), carve(FA, 12296, [128, D])]

        def blk_info(b):
            is_s = b == 8
            n = 128 if is_s else 512
            c0 = S if is_s else b * 512
            xsrc = xs.ap() if is_s else xp[b * 512:(b + 1) * 512, :]
            return is_s, n, c0, xsrc

        def prefetch_x_dma(b, t):
            _, n, c0, xsrc = blk_info(b)
            P.dma("sp", stg[t % 2], xsrc[t * 128:(t + 1) * 128, :])

        def prefetch_inputs(b):
            _, n, c0, xsrc = blk_info(b)
            nt = n // 128
            P.dma("sp", onb[:, :, 0:n], ond[0:8, :, c0:c0 + n].rearrange("u p c -> p u c"))
            P.dma("sp", obb[:, :, 0:n], ond[8:16, :, c0:c0 + n].rearrange("u p c -> p u c"))
            for t in range(nt):
                if t >= 2:
                    prefetch_x_dma(b, t)
                transpose_tile(stg[t % 2], xTb, t * 128)

        if n_cblocks > 0:
            for t in range(2):
                prefetch_x_dma(0, t)
            prefetch_inputs(0)
        for b in range(n_cblocks):
            is_s, n, c0, xsrc = blk_info(b)
            nt = n // 128
            ydst = y_s.ap() if is_s else y_p[b * 512:(b + 1) * 512, :]
            P.dma("sp", xres[:, 0:nt, :], xsrc.rearrange("(t p) d -> p t d", p=128))
            if b + 1 < n_cblocks:
                for t in range(min(2, blk_info(b + 1)[1] // 128)):
                    prefetch_x_dma(b + 1, t)
            for hf in range(2):
                wga = w_get()
                wpa = w_get()
                wgb = w_get()
                wpb = w_get()
                for fl in range(4):
                    fo = hf * 4 + fl
                    c = slice(fl * 128, fl * 128 + 128)
                    bk = nb()
                    for kc in range(8):
                        P.mm(bk[:, :n], wga[:, kc, c], xTb[:, kc, 0:n], start=(kc == 0), stop=(kc == 7))
                    P.act(sgA[:, :n], bk[:, :n], AF.Sigmoid)
                    bk = nb()
                    for kc in range(8):
                        P.mm(bk[:, :n], wpa[:, kc, c], onb[:, kc, 0:n], start=(kc == 0), stop=(kc == 7))
                    P.tt("dve", m1[:, :n], bk[:, :n], sgA[:, :n], ALU.mult)
                    bk = nb()
                    for kc in range(8):
                        P.mm(bk[:, :n], wgb[:, kc, c], xTb[:, kc, 0:n], start=(kc == 0), stop=(kc == 7))
                    P.act(sgB[:, :n], bk[:, :n], AF.Sigmoid)
                    bk = nb()
                    for kc in range(8):
                        P.mm(bk[:, :n], wpb[:, kc, c], obb[:, kc, 0:n], start=(kc == 0), stop=(kc == 7))
                    P.tt("dve", sgB[:, :n], bk[:, :n], sgB[:, :n], ALU.mult)
                    P.tt("pool", mT[:, fo, 0:n], m1[:, :n], sgB[:, :n], ALU.add)
            if b + 1 < n_cblocks:
                prefetch_inputs(b + 1)
            wo = [w_get(), w_get()]
            for t in range(nt):
                for cg in range(2):
                    bk = nb()
                    for kc in range(8):
                        P.mm(bk[:, :], mT[:, kc, t * 128:(t + 1) * 128], wo[cg][:, kc, :],
                             start=(kc == 0), stop=(kc == 7))
                    xs_ = xres[:, t, cg * 512:(cg + 1) * 512]
                    P.stt("dve", xs_, xs_, ALPHA, bk[:], ALU.mult, ALU.add)
                ln_stats(xres[:, t, :], t)
            ln_rstd(nt)
            for t in range(nt):
                ln_apply(xres[:, t, :], 0, t)
                transpose_tile(xres[:, t, :], hT, t * 128)
            need_conv = b >= 7
            if is_s:
                segs = [(0, 64, 0), (64, 64, 1)]
            else:
                segs = [(0, 512, None)]
            for ci in range(6):
                w = 512 if ci < 5 else 256
                wu = w_get()
                wg_ = w_get()
                if need_conv:
                    bk = nb()
                    for kc in range(8):
                        P.mm(bk[:, :w], hT[:, kc, n - 128:n], wu[:, kc, 0:w], start=(kc == 0), stop=(kc == 7))
                    P.copy("dve", utok[:, ci * 512:ci * 512 + w], bk[:, :w])
                for fl in range(w // 128):
                    fo = ci * 4 + fl
                    c = slice(fl * 128, fl * 128 + 128)
                    bu = nb()
                    for kc in range(8):
                        P.mm(bu[:, :n], wu[:, kc, c], hT[:, kc, 0:n], start=(kc == 0), stop=(kc == 7))
                    bg = nb()
                    for kc in range(8):
                        P.mm(bg[:, :n], wg_[:, kc, c], hT[:, kc, 0:n], start=(kc == 0), stop=(kc == 7))
                    uo = 0
                    for (sc0, sl, ss) in segs:
                        ub = ubuf[:, uo:uo + sl + 2]
                        uo += sl + 2
                        halo = hal[:, fo, :] if ss is None else cch[ss][:, fo, :]
                        P.copy("pool", ub[:, 0:2], halo)
                        P.copy("act", ub[:, 2:2 + sl], bu[:, sc0:sc0 + sl])
                        cv = cva[:, sc0:sc0 + sl]
                        P.ts("dve", cv, ub[:, 0:sl], convp_s[:, fo, 0:1], convp_s[:, fo, 3:4], ALU.mult, ALU.add)
                        P.stt("dve", cv, ub[:, 1:1 + sl], convp_s[:, fo, 1:2], cv, ALU.mult, ALU.add)
                        P.stt("dve", cv, ub[:, 2:2 + sl], convp_s[:, fo, 2:3], cv, ALU.mult, ALU.add)
                        if ss is None and b < NBLK - 1:
                            P.copy("pool", hal[:, fo, :], ub[:, sl:sl + 2])
                    P.act(cvg[:, :n], cva[:, :n], AF.Gelu_apprx_tanh)
                    P.tt("dve", zT[:, fo, 0:n], cvg[:, :n], bg[:, :n], ALU.mult)
            if need_conv:
                if is_s:
                    for s in range(2):
                        P.dma("sp", oconv_s[2 * s:2 * s + 2, :], utok[64 * s + 62:64 * s + 64, :])
                else:
                    P.dma("sp", oconv_p.ap(), utok[126:128, :])
            for cg in range(2):
                banks = [nb() for _ in range(nt)]
                for kg in range(3):
                    wd = w_get()
                    nkc = 8 if kg < 2 else 6
                    for t in range(nt):
                        for kl in range(nkc):
                            kc = kg * 8 + kl
                            P.mm(banks[t][:, :], zT[:, kc, t * 128:(t + 1) * 128], wd[:, kl, :],
                                 start=(kc == 0), stop=(kc == NFC - 1))
                for t in range(nt):
                    xs_ = xres[:, t, cg * 512:(cg + 1) * 512]
                    P.stt("dve", xs_, xs_, ALPHA, banks[t][:], ALU.mult, ALU.add)
                    if cg == 1:
                        ln_stats(xres[:, t, :], t)
            ln_rstd(nt)
            for t in range(nt):
                ln_apply(xres[:, t, :], 2, t)
            P.dma("pool", ydst.rearrange("(t p) d -> p t d", p=128), xres[:, 0:nt, :])

        P.emit()
        build_nc.stats = (len(P.ops), P.n_waits)
    return nc


def _t5_bucket(rel):
    nb_, me = 16, 8
    ret = np.where(rel > 0, nb_, 0)
    n = np.abs(rel)
    nf = np.maximum(n, 1).astype(np.float32)
    large = me + (np.log(nf / np.float32(me)) / np.float32(math.log(128 / me)) * np.float32(nb_ - me)).astype(np.int32)
    large = np.minimum(large, nb_ - 1)
    return ret + np.where(n < me, n, large)


_NC_CACHE = {}


def kernel(**inputs):
    f32 = np.float32
    g = {k: np.asarray(v) for k, v in inputs.items()}
    p = np.arange(128)[:, None]
    mcol = np.arange(256)[None, :]
    rel = p - mcol
    ga = g["t5_table"][_t5_bucket(rel)]
    gb = g["rel_table_b"][0][np.clip(rel, -128, 128) + 128]
    gtab = np.ascontiguousarray(np.concatenate([ga, gb], axis=2).transpose(0, 2, 1)).astype(f32)
    ca = g["t5_table"][_t5_bucket(np.array([-1000]))][0]
    cb = g["rel_table_b"][0][0]
    ctab = np.ascontiguousarray(np.broadcast_to(np.concatenate([ca, cb])[None, :], (128, 24))).astype(f32)
    lamv = np.stack([g["lambda_q1"][0], g["lambda_k1"][0], g["lambda_q2"][0], g["lambda_k2"][0]], 0)
    lamv = np.ascontiguousarray(np.broadcast_to(lamv[None], (128, 4, 64))).astype(f32)
    sublng = np.ascontiguousarray(g["subln_g"][0].reshape(128, 1)).astype(f32)
    lntab = np.stack([g["ln1_g"][0], g["ln1_b"][0], g["ln2_g"][0], g["ln2_b"][0]], 0)
    lntab = np.ascontiguousarray(np.broadcast_to(lntab[None], (128, 4, D))).astype(f32)
    cvp = np.concatenate([g["conv_w"][0], g["conv_b"][0][None]], 0)
    convp = np.ascontiguousarray(cvp.reshape(4, NFC, 128).transpose(2, 1, 0)).astype(f32)
    idn = np.eye(128, dtype=f32)
    shared = {
        "w_in": np.ascontiguousarray(g["w_in"][0]), "w_pa": np.ascontiguousarray(g["w_pa"][0]),
        "w_pb": np.ascontiguousarray(g["w_pb"][0]), "w_out": np.ascontiguousarray(g["w_out"][0]),
        "w_up": np.ascontiguousarray(g["w_up"][0]), "w_gate": np.ascontiguousarray(g["w_gate"][0]),
        "w_down": np.ascontiguousarray(g["w_down"][0]),
        "lamv": lamv, "sublng": sublng, "lntab": lntab, "convp": convp, "gtab": gtab, "ctab": ctab, "idn": idn,
    }
    in_maps = []
    for c in range(N_CORES):
        sl = slice(2 * c, 2 * c + 2)
        m = dict(shared)
        m["xp"] = np.ascontiguousarray(g["x_prompt"][c])
        m["xs"] = np.ascontiguousarray(g["x_sample"][sl].reshape(2 * TS, D))
        m["cak"] = np.ascontiguousarray(g["cache_a_k"][0, sl].reshape(2, 2048, D))
        m["cav"] = np.ascontiguousarray(g["cache_a_v"][0, sl].reshape(2, 2048, D))
        m["cbk"] = np.ascontiguousarray(g["cache_b_k"][0, sl].reshape(2, 512, D))
        m["cbv"] = np.ascontiguousarray(g["cache_b_v"][0, sl].reshape(2, 512, D))
        m["cconv"] = np.ascontiguousarray(g["cache_conv"][0, sl])
        in_maps.append(m)
    if "nc" not in _NC_CACHE:
        _NC_CACHE["nc"] = build_nc()
    nc = _NC_CACHE["nc"]
    res = run_bass_kernel_spmd(nc, in_maps, core_ids=list(range(N_CORES)))
    R = res.results

    def cat(name):
        return np.stack([np.asarray(R[c][name]) for c in range(N_CORES)], 0)

    y_p = cat("y_p")
    y_s = cat("y_s").reshape(16, TS, D)
    oak_p = cat("oak_p").reshape(1, 8, S, 8, 2, 64)
    oav_p = cat("oav_p").reshape(1, 8, S, 8, 128)
    obk_p = cat("obk_p").reshape(1, 8, 512, 16, 64)
    obv_p = cat("obv_p").reshape(1, 8, 512, 16, 64)
    oconv_p = cat("oconv_p").reshape(1, 8, 2, DFF)
    oak_s = cat("oak_s").reshape(1, 16, TS, 8, 2, 64)
    oav_s = cat("oav_s").reshape(1, 16, TS, 8, 128)
    obk_s = cat("obk_s").reshape(1, 16, TS, 16, 64)
    obv_s = cat("obv_s").reshape(1, 16, TS, 16, 64)
    oconv_s = cat("oconv_s").reshape(1, 16, 2, DFF)
    outs = (y_p, y_s, oak_p, oav_p, obk_p, obv_p, oconv_p, oak_s, oav_s, obk_s, obv_s, oconv_s)
    return tuple(np.ascontiguousarray(o, dtype=f32) for o in outs)
```

```python
import math
import contextlib
import numpy as np
import concourse.bass as bass
import concourse.mybir as mybir
from concourse.bass_utils import run_bass_kernel_spmd

F32 = mybir.dt.float32
BF16 = mybir.dt.bfloat16
AF = mybir.ActivationFunctionType
ALU = mybir.AluOpType
AX = mybir.AxisListType

N_CORES = 8
D = 1024
S = 4096
NBLK = 8
TS = 64
DFF = 2816
NFC = 22
ALPHA = 2.0 ** 0.25
LAM_INIT = 0.2
EPS = 1e-5
NTOK = S + 2 * TS

COMPUTE = ("pe", "act", "dve", "pool")
N_DMA_SLOTS = {"sp": 12, "pool": 6}


def ap_box(ap):
    t = ap.tensor
    name = t.name
    pat = [list(x) for x in ap.ap]
    off = int(ap.offset)
    if "DRam" in type(t).__name__:
        lo = hi = off
        for st, n in pat:
            if n > 1:
                if st >= 0:
                    hi += st * (n - 1)
                else:
                    lo += st * (n - 1)
        return (name, 0, 0, lo, hi)
    fsz = 1
    for s_ in list(t.shape)[1:]:
        fsz *= s_
    pstep, pn = pat[0]
    p0 = off // fsz
    foff = off - p0 * fsz
    ps = pstep // fsz
    lo = hi = foff
    for st, n in pat[1:]:
        if n > 1:
            if st >= 0:
                hi += st * (n - 1)
            else:
                lo += st * (n - 1)
    return (name, p0, p0 + (pn - 1) * ps, lo, hi)


def boxes_overlap(a, b):
    return not (a[2] < b[1] or b[2] < a[1] or a[4] < b[3] or b[4] < a[3])


def box_contains(a, b):
    return a[1] <= b[1] and a[2] >= b[2] and a[3] <= b[3] and a[4] >= b[4]


class Op:
    __slots__ = ("eng", "fn", "reads", "writes", "dma", "idx", "deps", "sig", "slot", "slotval", "signals")

    def __init__(self, eng, fn, reads, writes, dma):
        self.eng = eng
        self.fn = fn
        self.reads = reads
        self.writes = writes
        self.dma = dma
        self.deps = set()
        self.sig = None
        self.slot = None
        self.slotval = None
        self.signals = False


class Prog:
    def __init__(self, nc):
        self.nc = nc
        self.ops = []
        self.hist = {}

    def op(self, eng, fn, reads=(), writes=(), dma=False):
        rb, wb = [], []
        for a in reads:
            if "PSum" in type(a.tensor).__name__:
                wb.append((a.tensor.name, 0, 127, 0, 1 << 30))
            else:
                rb.append(ap_box(a))
        for a in writes:
            if "PSum" in type(a.tensor).__name__:
                wb.append((a.tensor.name, 0, 127, 0, 1 << 30))
            else:
                wb.append(ap_box(a))
        o = Op(eng, fn, rb, wb, dma)
        o.idx = len(self.ops)
        self.ops.append(o)
        ops = self.ops
        for b in o.reads:
            for (hb, hi, hw) in self.hist.setdefault(b[0], []):
                if hw and boxes_overlap(hb, b):
                    o.deps.add(hi)
        for b in o.writes:
            for (hb, hi, hw) in self.hist.setdefault(b[0], []):
                if boxes_overlap(hb, b):
                    o.deps.add(hi)
        o.deps.discard(o.idx)
        for b in o.writes:
            h = self.hist[b[0]]
            h[:] = [e for e in h if not box_contains(b, e[0])]
            h.append((b, o.idx, True))
        for b in o.reads:
            h = self.hist[b[0]]
            if not o.dma:
                h[:] = [e for e in h if not ((not e[2]) and e[0] == b and ops[e[1]].eng == o.eng
                                             and not ops[e[1]].dma)]
            h.append((b, o.idx, False))
        return o

    def emit(self):
        nc = self.nc
        ops = self.ops
        for o in ops:
            for d in list(o.deps):
                p = ops[d]
                if (not p.dma) and (not o.dma) and p.eng == o.eng and p.eng == "pe":
                    o.deps.discard(d)
                    continue
                p.signals = True
        cnt = {e: 0 for e in COMPUTE}
        slot_rr = {q: 0 for q in N_DMA_SLOTS}
        slot_cnt = {q: [0] * n for q, n in N_DMA_SLOTS.items()}
        for o in ops:
            if o.dma:
                q = o.eng
                s = slot_rr[q]
                slot_rr[q] = (s + 1) % N_DMA_SLOTS[q]
                slot_cnt[q][s] += 16
                o.slot = (q, s)
                o.slotval = slot_cnt[q][s]
            elif o.signals:
                cnt[o.eng] += 1
                o.sig = cnt[o.eng]
        with contextlib.ExitStack() as es:
            sems = {e: es.enter_context(nc.semaphore("s_" + e)) for e in COMPUTE}
            dsems = {q: [es.enter_context(nc.semaphore("d_%s%d" % (q, i))) for i in range(n)]
                     for q, n in N_DMA_SLOTS.items()}
            block = es.enter_context(nc.Block())
            by_eng = {}
            for o in ops:
                by_eng.setdefault(o.eng, []).append(o)
            self.n_waits = 0

            def run_engine(ename, eng):
                waited = {}

                def wait(key, sem, val):
                    if waited.get(key, 0) >= val:
                        return
                    eng.wait_ge(sem, val)
                    self.n_waits += 1
                    waited[key] = val

                for o in by_eng.get(ename, []):
                    need = {}
                    for d in o.deps:
                        p = ops[d]
                        if p.dma:
                            k = ("d",) + p.slot
                            need[k] = max(need.get(k, 0), p.slotval)
                        else:
                            k = ("c", p.eng)
                            need[k] = max(need.get(k, 0), p.sig)
                    if o.dma:
                        k = ("d",) + o.slot
                        need[k] = max(need.get(k, 0), o.slotval - 16)
                    for k, v in sorted(need.items()):
                        if v <= 0:
                            continue
                        if k[0] == "c":
                            wait(k, sems[k[1]], v)
                        else:
                            wait(k, dsems[k[1]][k[2]], v)
                    ins = o.fn(eng)
                    if o.dma:
                        ins.then_inc(dsems[o.slot[0]][o.slot[1]], 16)
                    elif o.signals:
                        ins.then_inc(sems[o.eng], 1)
                if ename == "sp":
                    for q, n in N_DMA_SLOTS.items():
                        for s in range(n):
                            if slot_cnt[q][s] > 0:
                                wait(("d", q, s), dsems[q][s], slot_cnt[q][s])

            @block.tensor
            def _(e):
                run_engine("pe", e)

            @block.scalar
            def _(e):
                run_engine("act", e)

            @block.vector
            def _(e):
                run_engine("dve", e)

            @block.gpsimd
            def _(e):
                run_engine("pool", e)

            @block.sync
            def _(e):
                run_engine("sp", e)

    def dma(self, q, out, in_):
        return self.op(q, lambda e: e.dma_start(out=out, in_=in_), reads=[in_], writes=[out], dma=True)

    def mm(self, out, lhsT, rhs, start=True, stop=True):
        return self.op("pe", lambda e: e.matmul(out, lhsT, rhs, start=start, stop=stop),
                       reads=[lhsT, rhs], writes=[out])

    def transpose(self, out, in_, ident):
        return self.op("pe", lambda e: e.transpose(out, in_, ident), reads=[in_, ident], writes=[out])

    def act(self, out, in_, func, bias=None, scale=1.0):
        kw = {}
        rd = [in_]
        if bias is not None:
            kw["bias"] = bias
            if not isinstance(bias, (int, float)):
                rd.append(bias)
        return self.op("act", lambda e: e.activation(out, in_, func, scale=scale, **kw), reads=rd, writes=[out])

    def tt(self, eng, out, in0, in1, op):
        return self.op(eng, lambda e: e.tensor_tensor(out, in0, in1, op), reads=[in0, in1], writes=[out])

    def ts(self, eng, out, in0, s1, s2, op0, op1=None):
        rd = [in0] + [s for s in (s1, s2) if s is not None and not isinstance(s, (int, float))]
        if op1 is None:
            return self.op(eng, lambda e: e.tensor_scalar(out, in0, s1, s2, op0), reads=rd, writes=[out])
        return self.op(eng, lambda e: e.tensor_scalar(out, in0, s1, s2, op0, op1), reads=rd, writes=[out])

    def stt(self, eng, out, in0, scalar, in1, op0, op1):
        rd = [in0, in1] + ([] if isinstance(scalar, (int, float)) else [scalar])
        return self.op(eng, lambda e: e.scalar_tensor_tensor(out, in0, scalar, in1, op0, op1), reads=rd, writes=[out])

    def copy(self, eng, out, in_):
        if eng == "act":
            return self.op(eng, lambda e: e.copy(out, in_), reads=[in_], writes=[out])
        return self.op(eng, lambda e: e.tensor_copy(out, in_), reads=[in_], writes=[out])

    def memset(self, eng, out, val):
        return self.op(eng, lambda e: e.memset(out, val), reads=[], writes=[out])

    def recip(self, out, in_):
        return self.op("dve", lambda e: e.reciprocal(out, in_), reads=[in_], writes=[out])

    def recip_fast(self, out, in_):
        return self.op("dve", lambda e: e.reciprocal_approx_fast(out, in_), reads=[in_], writes=[out])


def carve(T, off, shape):
    n = 1
    for s_ in shape[1:]:
        n *= s_
    ap = T[:, off:off + n]
    if len(shape) == 3:
        ap = ap.rearrange("p (a b) -> p a b", a=shape[1])
    return ap


def v3(ap, a):
    return ap.rearrange("p (a b) -> p a b", a=a)


def build_nc(n_units=16, n_cblocks=9):
    nc = bass.Bass("TRN2", target_bir_lowering=False)

    def di(n, s):
        return nc.dram_tensor(n, s, F32, kind="ExternalInput")

    def do(n, s):
        return nc.dram_tensor(n, s, F32, kind="ExternalOutput")

    xp = di("xp", [S, D])
    xs = di("xs", [2 * TS, D])
    cak = di("cak", [2, 2048, D])
    cav = di("cav", [2, 2048, D])
    cbk = di("cbk", [2, 512, D])
    cbv = di("cbv", [2, 512, D])
    cconv = di("cconv", [2, 2, DFF])
    w_in = di("w_in", [D, 8192])
    w_pa = di("w_pa", [D, D])
    w_pb = di("w_pb", [D, D])
    w_out = di("w_out", [D, D])
    w_up = di("w_up", [D, DFF])
    w_gate = di("w_gate", [D, DFF])
    w_down = di("w_down", [DFF, D])
    lamv = di("lamv", [128, 4, 64])
    sublng = di("sublng", [128, 1])
    lntab = di("lntab", [128, 4, D])
    convp = di("convp", [128, NFC, 4])
    gtab = di("gtab", [128, 24, 256])
    ctab_d = di("ctab", [128, 24])
    idn = di("idn", [128, 128])

    y_p = do("y_p", [S, D])
    y_s = do("y_s", [2 * TS, D])
    oak_p = do("oak_p", [S, D])
    oav_p = do("oav_p", [S, D])
    obk_p = do("obk_p", [512, D])
    obv_p = do("obv_p", [512, D])
    oconv_p = do("oconv_p", [2, DFF])
    oak_s = do("oak_s", [2 * TS, D])
    oav_s = do("oav_s", [2 * TS, D])
    obk_s = do("obk_s", [2 * TS, D])
    obv_s = do("obv_s", [2 * TS, D])
    oconv_s = do("oconv_s", [4, DFF])

    ond = nc.dram_tensor("ond", [16, 128, NTOK], BF16)
    wg_b = nc.dram_tensor("wg_b", [D, 2048], BF16)
    wpa_b = nc.dram_tensor("wpa_b", [D, D], BF16)
    wpb_b = nc.dram_tensor("wpb_b", [D, D], BF16)
    wo_b = nc.dram_tensor("wo_b", [D, D], BF16)
    wup_b = nc.dram_tensor("wup_b", [D, DFF], BF16)
    wgt_b = nc.dram_tensor("wgt_b", [D, DFF], BF16)
    wdn_b = nc.dram_tensor("wdn_b", [DFF, D], BF16)

    with contextlib.ExitStack() as es:
        def sb(n, s, d):
            return es.enter_context(nc.sbuf_tensor(n, s, d))

        AR = sb("AR", [128, 65536], BF16)
        FA = sb("FA", [128, 14336], F32)
        ET = [sb("ET%d" % i, [128, 512], BF16) for i in range(6)]
        ident = sb("ident", [128, 128], F32)
        onesb = sb("onesb", [128, 128], BF16)
        onesf = sb("onesf", [128, 128], F32)
        epsb = sb("epsb", [128, 1], F32)
        lamt = sb("lamt", [128, 8], F32)
        ctab = sb("ctab_s", [128, 24], F32)
        lamv_s = sb("lamv_s", [128, 4, 64], F32)
        convp_s = sb("convp_s", [128, NFC, 4], F32)
        hal = sb("hal", [128, NFC, 2], F32)
        cch = [sb("cch%d" % s, [128, NFC, 2], F32) for s in range(2)]
        bst = sb("bst", [128, 4, 12], F32)
        mv = sb("mv", [128, 4, 2], F32)
        rstd = sb("rstd", [128, 4], F32)
        BK = [es.enter_context(nc.psum_tensor("BK%d" % i, [128, 512], F32)) for i in range(8)]

        P = Prog(nc)
        rows = [slice(0, 64), slice(64, 128)]

        xT = carve(AR, 0, [128, 8, NTOK])
        QT = carve(AR, 33792, [128, NTOK])
        KT = carve(AR, 38016, [128, NTOK])
        Vt = carve(AR, 42240, [128, 32, 128])
        Vs = carve(AR, 46336, [128, 2, 128])
        OnT = carve(AR, 46592, [128, NTOK])
        Wh = [carve(AR, 50816, [128, 8, 384]), carve(AR, 53888, [128, 8, 384])]
        ckT = [carve(AR, 56960, [128, 2048]), carve(AR, 59008, [128, 2048])]
        cV = [carve(AR, 61056, [128, 16, 128]), carve(AR, 63104, [128, 16, 128])]
        gt = [carve(FA, 0, [128, 2, 256]), carve(FA, 512, [128, 2, 256])]
        kvst = [carve(FA, 1024, [128, 4, 256]), carve(FA, 2048, [128, 4, 256])]
        tmpn = [carve(FA, 3072, [128, 512]), carve(FA, 3584, [128, 512]), carve(FA, 10304, [128, 512])]
        fb = [carve(FA, 4096 + 512 * i, [128, 512]) for i in range(10)]
        ckf = [carve(FA, 9216, [128, 4, 128]), carve(FA, 9728, [128, 4, 128])]
        lprod = carve(FA, 10240, [128, 64])
        fbo = [carve(FA, 10816, [128, 512]), carve(FA, 11328, [128, 512])]

        STB = [BK[0], BK[1], BK[6]]
        OTB = [BK[2], BK[3]]
        SMB = [BK[4], BK[5]]
        PJ = [BK[7], BK[0], BK[1], BK[6]]
        cnt = {"pj": 0, "item": 0, "kv": 0, "ckf": 0, "nb": 0, "fbo": 0}

        def pj():
            cnt["pj"] += 1
            return PJ[cnt["pj"] % 4]

        P.dma("sp", ident[:], idn.ap())
        P.dma("sp", lamv_s[:], lamv.ap())
        P.dma("sp", ctab[:], ctab_d.ap())
        P.dma("sp", convp_s[:], convp.ap())
        P.dma("sp", lamt[:, 5:6], sublng.ap())
        P.memset("dve", onesb[:], 1.0)
        P.memset("dve", onesf[:], 1.0)
        P.memset("dve", epsb[:], EPS)
        P.memset("dve", hal[:], 0.0)
        for i in range(2):
            P.tt("dve", lprod, lamv_s[:, 2 * i, :], lamv_s[:, 2 * i + 1, :], ALU.mult)
            P.op("dve", lambda e, i=i: e.reduce_sum(lamt[:, i:i + 1], lprod, AX.X), reads=[lprod],
                 writes=[lamt[:, i:i + 1]])
        P.act(lamt[:, 2:4], lamt[:, 0:2], AF.Exp)
        P.tt("dve", lamt[:, 4:5], lamt[:, 3:4], lamt[:, 2:3], ALU.subtract)
        P.ts("dve", lamt[:, 4:5], lamt[:, 4:5], -LAM_INIT, None, ALU.add)
        P.ts("dve", lamt[:, 5:6], lamt[:, 5:6], 1.0 - LAM_INIT, None, ALU.mult)
        neglam = lamt[:, 4:5]
        gsub = lamt[:, 5:6]

        def load_unit_weights(u):
            isA = u < 8
            qc0 = (0 if isA else 3072) + (u % 8) * 128
            W = Wh[u % 2]
            for i, c0 in enumerate((qc0, qc0 + 1024, qc0 + 2048)):
                P.dma("pool", W[:, :, i * 128:(i + 1) * 128],
                      w_in[:, c0:c0 + 128].rearrange("(kc p) n -> p kc n", p=128))
            if isA:
                P.dma("sp", gt[u % 2][:, 0:1, :], gtab[:, u:u + 1, :])
            else:
                j = u - 8
                P.dma("sp", gt[u % 2][:, 0:2, :], gtab[:, 8 + 2 * j:10 + 2 * j, :])

        load_unit_weights(0)

        wcasts = []
        for r0 in range(0, D, 256):
            wcasts.append((wg_b[r0:r0 + 256, :], w_in[r0:r0 + 256, 6144:8192]))
            wcasts.append((wpa_b[r0:r0 + 256, :], w_pa[r0:r0 + 256, :]))
            wcasts.append((wpb_b[r0:r0 + 256, :], w_pb[r0:r0 + 256, :]))
            wcasts.append((wo_b[r0:r0 + 256, :], w_out[r0:r0 + 256, :]))
            wcasts.append((wup_b[r0:r0 + 256, :], w_up[r0:r0 + 256, :]))
            wcasts.append((wgt_b[r0:r0 + 256, :], w_gate[r0:r0 + 256, :]))
        for r0 in range(0, DFF, 256):
            wcasts.append((wdn_b[r0:r0 + 256, :], w_down[r0:r0 + 256, :]))
        wc_i = [0]

        def emit_wcasts(k):
            for _ in range(k):
                if wc_i[0] < len(wcasts):
                    d_, s_ = wcasts[wc_i[0]]
                    P.dma("pool", d_, s_)
                    wc_i[0] += 1

        def transpose_tile(src, dstT, col0, width=128):
            for g in range(2):
                bank = BK[cnt["nb"] % 8]
                cnt["nb"] += 1
                for i in range(4):
                    kc = g * 4 + i
                    P.transpose(bank[:, i * 128:(i + 1) * 128], src[:, kc * 128:(kc + 1) * 128], ident[:])
                P.copy("act" if g == 0 else "dve", dstT[:, 4 * g:4 * g + 4, col0:col0 + 128], v3(bank[:], 4))

        for t in range(33):
            stage = kvst[t % 2].rearrange("p a b -> p (a b)")
            src = xp[t * 128:(t + 1) * 128, :] if t < 32 else xs.ap()
            P.dma("sp", stage, src)
            transpose_tile(stage, xT, t * 128)

        def attn(tiles, qcol0, tsel, csel):
            items = [(ti, m) for ti in range(len(tiles)) for m in range(2)]
            pend = []
            LOOK = 2
            DEFER_AT = 8
            n_done = [0]

            def flush_one():
                T, m, et, ti = pend.pop(0)
                nk, a, b = T["nk"], T["c_lo"], T["c_hi"]
                first = ti == 0
                last = ti == len(tiles) - 1
                P.mm(OTB[m][:, a:b], T["V"], et[0:nk, a:b], start=first, stop=last)
                P.mm(SMB[m][:, a:b], onesb[0:nk, :], et[0:nk, a:b], start=first, stop=last)

            for (ti, m) in items:
                T = tiles[ti]
                k = cnt["item"]
                cnt["item"] += 1
                st = STB[k % 3]
                et = ET[k % 6]
                tm = tmpn[k % 3]
                nk, a, b = T["nk"], T["c_lo"], T["c_hi"]
                qa = qcol0 + (a - T["cbase"])
                P.mm(st[0:nk, a:b], T["kT"](m), QT[rows[m], qa:qa + (b - a)])
                for (pa, pb, kind, tc0) in T["parts"]:
                    if kind == "far":
                        P.act(et[0:nk, pa:pb], st[0:nk, pa:pb], AF.Exp,
                              bias=ctab[0:nk, csel(m):csel(m) + 1], scale=0.125)
                    else:
                        P.stt("dve", tm[0:nk, pa:pb], st[0:nk, pa:pb], 0.125,
                              T["gt"][0:nk, tsel(m), tc0:tc0 + (pb - pa)], ALU.mult, ALU.add)
                        P.act(et[0:nk, pa:pb], tm[0:nk, pa:pb], AF.Exp)
                if T["corner"] is not None:
                    r0, r1, c0, c1 = T["corner"]
                    P.memset("dve", et[r0:r1, c0:c1], 0.0)
                pend.append((T, m, et, ti))
                if len(pend) > LOOK:
                    flush_one()
                if n_done[0] == DEFER_AT:
                    run_deferred()
                n_done[0] += 1
            while pend:
                flush_one()
            run_deferred()

        deferred = []

        def run_deferred():
            while deferred:
                deferred.pop(0)()

        def finalize(isA, c0, n, dst):
            cs = slice(c0, c0 + n)
            if isA:
                fo_ = fbo[cnt["fbo"] % 2]
                cnt["fbo"] += 1
                P.copy("act", fb[6][:, :n], SMB[0][:, cs])
                P.copy("dve", fb[7][:, :n], OTB[0][:, cs])
                P.copy("act", fb[8][:, :n], SMB[1][:, cs])
                P.copy("dve", fb[9][:, :n], OTB[1][:, cs])
                P.recip(fb[0][:, :n], fb[6][:, :n])
                P.tt("dve", fb[1][:, :n], fb[7][:, :n], fb[0][:, :n], ALU.mult)
                P.recip(fb[2][:, :n], fb[8][:, :n])
                P.tt("dve", fb[3][:, :n], fb[9][:, :n], fb[2][:, :n], ALU.mult)
                P.stt("dve", fo_[:, :n], fb[3][:, :n], neglam, fb[1][:, :n], ALU.mult, ALU.add)
                P.tt("dve", fb[5][:, :n], fo_[:, :n], fo_[:, :n], ALU.mult)

                def tail(fo_=fo_, n=n, dst=dst):
                    bank = BK[7]
                    P.mm(bank[:, :n], onesf[:], fb[5][:, :n])
                    P.act(fb[4][:, :n], bank[:, :n], AF.Sqrt, bias=epsb[:], scale=1.0 / 128)
                    P.recip(fb[4][:, :n], fb[4][:, :n])
                    P.stt("dve", dst, fo_[:, :n], gsub, fb[4][:, :n], ALU.mult, ALU.mult)
                deferred.append(tail)
            else:
                for m in range(2):
                    r = rows[m]
                    P.copy("act", fb[6][r, :n], SMB[m][r, cs])
                    P.copy("dve", fb[7][r, :n], OTB[m][r, cs])
                P.recip(fb[0][:, :n], fb[6][:, :n])
                P.tt("dve", dst, fb[7][:, :n], fb[0][:, :n], ALU.mult)

        def prompt_tiles(isA, qb, g):
            def mk(kt, c_lo, c_hi, parts, corner):
                return dict(kT=lambda m, kt=kt: KT[rows[m], kt * 128:(kt + 1) * 128], V=Vt[:, kt, :], nk=128,
                            c_lo=c_lo, c_hi=c_hi, cbase=0, parts=parts, corner=corner, gt=g)

            def diag(i):
                c0 = 128 * i
                parts = [(c0, min(c0 + 256, 512), "near", 0)]
                if c0 + 256 < 512:
                    parts.append((c0 + 256, 512, "far", 0))
                return mk(4 * qb + i, c0, 512, parts, (64, 128, c0, c0 + 64))

            tiles = []
            if isA:
                for kt in range(0, 4 * qb - 1):
                    tiles.append(mk(kt, 0, 512, [(0, 512, "far", 0)], None))
                if qb > 0:
                    tiles.append(mk(4 * qb - 1, 0, 512, [(0, 128, "near", 128), (128, 512, "far", 0)], None))
                for i in range(4):
                    tiles.append(diag(i))
            else:
                tiles.append(diag(0))
                if qb > 0:
                    for ip in range(4):
                        c_hi = 128 * (ip + 1)
                        corner = (0, 64, 128 * ip + 64, 128 * ip + 128)
                        if ip < 3:
                            parts = [(0, c_hi, "far", 0)]
                        else:
                            parts = [(0, 128, "near", 128), (128, 512, "far", 0)]
                        tiles.append(mk(4 * qb - 4 + ip, 0, c_hi, parts, corner))
                for i in range(1, 4):
                    tiles.append(diag(i))
            return tiles

        def sample_tiles(isA, s, g):
            ncache = 16 if isA else 4
            cb = 64 * s
            tiles = []
            for t in range(ncache):
                parts = [(cb, cb + 64, "far", 0)] if t < ncache - 1 else [(cb, cb + 64, "near", 128)]
                tiles.append(dict(kT=lambda m, t=t: ckT[s][rows[m], t * 128:(t + 1) * 128], V=cV[s][:, t, :],
                                  nk=128, c_lo=cb, c_hi=cb + 64, cbase=cb, parts=parts, corner=None, gt=g))
            tiles.append(dict(kT=lambda m: KT[rows[m], S + 64 * s:S + 64 * s + 64], V=Vs[0:64, s, :], nk=64,
                              c_lo=cb, c_hi=cb + 64, cbase=cb, parts=[(cb, cb + 64, "near", 0)], corner=None,
                              gt=g))
            return tiles

        n_wc_per_unit = (len(wcasts) + 15) // 16
        for u in range(n_units):
            isA = u < 8
            hj = u % 8
            W = Wh[u % 2]
            g = gt[u % 2]
            if u + 1 < n_units:
                load_unit_weights(u + 1)
            emit_wcasts(n_wc_per_unit)
            colsK = slice(hj * 128, hj * 128 + 128)
            if isA:
                tsel = lambda m: 0
                csel = lambda m, hj=hj: hj
                okd, ovd, oks, ovs = oak_p, oav_p, oak_s, oav_s
                ck_src, cv_src, ncch = cak, cav, 4
            else:
                tsel = lambda m: m
                csel = lambda m, hj=hj: 8 + 2 * hj + m
                okd, ovd, oks, ovs = obk_p, obv_p, obk_s, obv_s
                ck_src, cv_src, ncch = cbk, cbv, 1
            for s in range(2):
                P.dma("pool", cV[s][:, 0:4 * ncch, :],
                      cv_src[s, :, colsK].rearrange("(t p) c -> p t c", p=128))
            for blk in range(NBLK):
                cols = slice(blk * 512, blk * 512 + 512)
                for wi, dst in ((0, QT), (1, KT)):
                    bank = pj()
                    for kc in range(8):
                        P.mm(bank[:, :], W[:, kc, wi * 128:(wi + 1) * 128], xT[:, kc, cols],
                             start=(kc == 0), stop=(kc == 7))
                    P.copy("act", dst[:, cols], bank[:])
                need_out = isA or blk == NBLK - 1
                if need_out:
                    st_ = kvst[cnt["kv"] % 2]
                    cnt["kv"] += 1
                    for half in range(2):
                        bank = pj()
                        for tt in range(2):
                            t = blk * 4 + half * 2 + tt
                            for kc in range(8):
                                P.mm(bank[:, tt * 256:(tt + 1) * 256], xT[:, kc, t * 128:(t + 1) * 128],
                                     W[:, kc, 128:384], start=(kc == 0), stop=(kc == 7))
                        P.copy("act", st_[:, half * 2:half * 2 + 2, :], v3(bank[:], 2))
                        P.copy("dve", Vt[:, blk * 4 + half * 2:blk * 4 + half * 2 + 2, :],
                               st_[:, half * 2:half * 2 + 2, 128:256])
                    r0 = blk * 512 if isA else 0
                    P.dma("sp", okd[r0:r0 + 512, colsK].rearrange("(t p) c -> p t c", p=128), st_[:, :, 0:128])
                    P.dma("sp", ovd[r0:r0 + 512, colsK].rearrange("(t p) c -> p t c", p=128), st_[:, :, 128:256])
                else:
                    bank = pj()
                    for tt in range(4):
                        t = blk * 4 + tt
                        for kc in range(8):
                            P.mm(bank[:, tt * 128:(tt + 1) * 128], xT[:, kc, t * 128:(t + 1) * 128],
                                 W[:, kc, 256:384], start=(kc == 0), stop=(kc == 7))
                    P.copy("dve", Vt[:, blk * 4:blk * 4 + 4, :], v3(bank[:], 4))
                attn(prompt_tiles(isA, blk, g), blk * 512, tsel, csel)
                finalize(isA, 0, 512, OnT[:, cols])
            cs_ = slice(S, S + 128)
            for wi, dst in ((0, QT), (1, KT)):
                bank = pj()
                for kc in range(8):
                    P.mm(bank[:, 0:128], W[:, kc, wi * 128:(wi + 1) * 128], xT[:, kc, cs_],
                         start=(kc == 0), stop=(kc == 7))
                P.copy("dve", dst[:, cs_], bank[:, 0:128])
            st_ = kvst[cnt["kv"] % 2]
            cnt["kv"] += 1
            bank = pj()
            for s in range(2):
                for kc in range(8):
                    P.mm(bank[0:64, s * 256:(s + 1) * 256], xT[:, kc, S + 64 * s:S + 64 * s + 64],
                         W[:, kc, 128:384], start=(kc == 0), stop=(kc == 7))
            P.copy("dve", st_[0:64, 0:2, :], v3(bank[0:64, :], 2))
            P.copy("dve", Vs[0:64, :, :], st_[0:64, 0:2, 128:256])
            P.dma("sp", oks[:, colsK].rearrange("(s t) c -> t s c", s=2), st_[0:64, 0:2, 0:128])
            P.dma("sp", ovs[:, colsK].rearrange("(s t) c -> t s c", s=2), st_[0:64, 0:2, 128:256])
            for s in range(2):
                for c4 in range(ncch):
                    cf = ckf[cnt["ckf"] % 2]
                    cnt["ckf"] += 1
                    P.dma("sp", cf[:], ck_src[s, c4 * 512:(c4 + 1) * 512, colsK].rearrange("(t p) c -> p t c", p=128))
                    bank = pj()
                    for i in range(4):
                        P.transpose(bank[:, i * 128:(i + 1) * 128], cf[:, i, :], ident[:])
                    P.copy("dve", ckT[s][:, c4 * 512:(c4 + 1) * 512], bank[:])
                attn(sample_tiles(isA, s, g), S + 64 * s, tsel, csel)
            finalize(isA, 0, 128, OnT[:, cs_])
            deferred.append(lambda u=u: P.dma("sp", ond[u], OnT))
        run_deferred()
        emit_wcasts(len(wcasts))

        NR = 7
        HELD = 4
        ring = [carve(AR, i * 4096, [128, 8, 512]) for i in range(NR)]
        zT = carve(AR, 28672, [128, NFC, 512])
        mT = carve(AR, 39936, [128, 8, 512])
        hT = carve(AR, 44032, [128, 8, 512])
        xTb = carve(AR, 48128, [128, 8, 512])
        onb = carve(AR, 52224, [128, 8, 512])
        obb = carve(AR, 56320, [128, 8, 512])
        xres = carve(FA, 0, [128, 4, D])
        lnt = carve(FA, 4096, [128, 4, D])
        sgA = carve(FA, 8192, [128, 512])
        sgB = carve(FA, 8704, [128, 512])
        m1 = carve(FA, 9216, [128, 512])
        ubuf = carve(FA, 9728, [128, 520])
        cva = carve(FA, 10248, [128, 512])
        cvg = carve(FA, 10760, [128, 512])
        utok = carve(FA, 11272, [128, DFF])

        def nb():
            b_ = BK[cnt["nb"] % 8]
            cnt["nb"] += 1
            return b_

        chunks = []
        for b in range(n_cblocks):
            for hf in range(2):
                c = slice(hf * 512, hf * 512 + 512)
                c2 = slice(1024 + hf * 512, 1024 + hf * 512 + 512)
                chunks += [(wg_b[:, c], 8, 512), (wpa_b[:, c], 8, 512), (wg_b[:, c2], 8, 512), (wpb_b[:, c], 8, 512)]
            chunks += [(wo_b[:, 0:512], 8, 512), (wo_b[:, 512:1024], 8, 512)]
            for ci in range(6):
                w = 512 if ci < 5 else 256
                chunks += [(wup_b[:, ci * 512:ci * 512 + w], 8, w), (wgt_b[:, ci * 512:ci * 512 + w], 8, w)]
            for cg in range(2):
                for kg in range(3):
                    nkc = 8 if kg < 2 else 6
                    chunks.append((wdn_b[kg * 1024:kg * 1024 + nkc * 128, cg * 512:(cg + 1) * 512], nkc, 512))
        wstate = {"issued": 0, "got": 0}

        def w_issue():
            k = wstate["issued"]
            if k >= len(chunks):
                return
            src, nkc, w = chunks[k]
            P.dma("sp", ring[k % NR][:, 0:nkc, 0:w], src.rearrange("(kc p) n -> p kc n", p=128))
            wstate["issued"] += 1

        def w_get():
            k = wstate["got"]
            wstate["got"] += 1
            while wstate["issued"] < min(len(chunks), k + NR - HELD + 1):
                w_issue()
            return ring[k % NR]

        if n_cblocks > 0:
            P.dma("sp", lnt, lntab.ap())
            ccs = FA[0:2, 11272:11272 + DFF]
            for s in range(2):
                P.dma("sp", ccs, cconv[s])
                bank = nb()
                for fo in range(NFC):
                    P.mm(bank[:, 2 * fo:2 * fo + 2], ccs[:, fo * 128:(fo + 1) * 128], ident[0:2, 0:2])
                P.copy("dve", cch[s][:], v3(bank[:, 0:2 * NFC], NFC))

        def ln_stats(xt, t):
            for c in range(2):
                P.op("dve", lambda e, c=c: e.bn_stats(bst[:, t, 6 * c:6 * c + 6], xt[:, c * 512:(c + 1) * 512]),
                     reads=[xt[:, c * 512:(c + 1) * 512]], writes=[bst[:, t, 6 * c:6 * c + 6]])
            P.op("dve", lambda e: e.bn_aggr(mv[:, t, :], bst[:, t, :]), reads=[bst[:, t, :]], writes=[mv[:, t, :]])

        def ln_rstd(nt):
            P.act(rstd[:, 0:nt], mv[:, 0:nt, 1], AF.Sqrt, bias=epsb[:], scale=1.0)
            P.recip(rstd[:, 0:nt], rstd[:, 0:nt])

        def ln_apply(xt, gi, t):
            P.ts("dve", xt, xt, mv[:, t, 0:1], rstd[:, t:t + 1], ALU.subtract, ALU.mult)
            P.tt("dve", xt, xt, lnt[:, gi, :], ALU.mult)
            P.tt("pool", xt, xt, lnt[:, gi + 1, :], ALU.add)

        stg = [carve(FA, 11272, [128, D]), carve(FA, 12296, [128, D])]

        def blk_info(b):
            is_s = b == 8
            n = 128 if is_s else 512
            c0 = S if is_s else b * 512
            xsrc = xs.ap() if is_s else xp[b * 512:(b + 1) * 512, :]
            return is_s, n, c0, xsrc

        def prefetch_x_dma(b, t):
            _, n, c0, xsrc = blk_info(b)
            P.dma("sp", stg[t % 2], xsrc[t * 128:(t + 1) * 128, :])

        def prefetch_inputs(b):
            _, n, c0, xsrc = blk_info(b)
            nt = n // 128
            P.dma("sp", onb[:, :, 0:n], ond[0:8, :, c0:c0 + n].rearrange("u p c -> p u c"))
            P.dma("sp", obb[:, :, 0:n], ond[8:16, :, c0:c0 + n].rearrange("u p c -> p u c"))
            for t in range(nt):
                if t >= 2:
                    prefetch_x_dma(b, t)
                transpose_tile(stg[t % 2], xTb, t * 128)

        if n_cblocks > 0:
            for t in range(2):
                prefetch_x_dma(0, t)
            prefetch_inputs(0)
        for b in range(n_cblocks):
            is_s, n, c0, xsrc = blk_info(b)
            nt = n // 128
            ydst = y_s.ap() if is_s else y_p[b * 512:(b + 1) * 512, :]
            P.dma("sp", xres[:, 0:nt, :], xsrc.rearrange("(t p) d -> p t d", p=128))
            if b + 1 < n_cblocks:
                for t in range(min(2, blk_info(b + 1)[1] // 128)):
                    prefetch_x_dma(b + 1, t)
            for hf in range(2):
                wga = w_get()
                wpa = w_get()
                wgb = w_get()
                wpb = w_get()
                for fl in range(4):
                    fo = hf * 4 + fl
                    c = slice(fl * 128, fl * 128 + 128)
                    bk = nb()
                    for kc in range(8):
                        P.mm(bk[:, :n], wga[:, kc, c], xTb[:, kc, 0:n], start=(kc == 0), stop=(kc == 7))
                    P.act(sgA[:, :n], bk[:, :n], AF.Sigmoid)
                    bk = nb()
                    for kc in range(8):
                        P.mm(bk[:, :n], wpa[:, kc, c], onb[:, kc, 0:n], start=(kc == 0), stop=(kc == 7))
                    P.tt("dve", m1[:, :n], bk[:, :n], sgA[:, :n], ALU.mult)
                    bk = nb()
                    for kc in range(8):
                        P.mm(bk[:, :n], wgb[:, kc, c], xTb[:, kc, 0:n], start=(kc == 0), stop=(kc == 7))
                    P.act(sgB[:, :n], bk[:, :n], AF.Sigmoid)
                    bk = nb()
                    for kc in range(8):
                        P.mm(bk[:, :n], wpb[:, kc, c], obb[:, kc, 0:n], start=(kc == 0), stop=(kc == 7))
                    P.tt("dve", sgB[:, :n], bk[:, :n], sgB[:, :n], ALU.mult)
                    P.tt("pool", mT[:, fo, 0:n], m1[:, :n], sgB[:, :n], ALU.add)
            if b + 1 < n_cblocks:
                prefetch_inputs(b + 1)
            wo = [w_get(), w_get()]
            for t in range(nt):
                for cg in range(2):
                    bk = nb()
                    for kc in range(8):
                        P.mm(bk[:, :], mT[:, kc, t * 128:(t + 1) * 128], wo[cg][:, kc, :],
                             start=(kc == 0), stop=(kc == 7))
                    xs_ = xres[:, t, cg * 512:(cg + 1) * 512]
                    P.stt("dve", xs_, xs_, ALPHA, bk[:], ALU.mult, ALU.add)
                ln_stats(xres[:, t, :], t)
            ln_rstd(nt)
            for t in range(nt):
                ln_apply(xres[:, t, :], 0, t)
                transpose_tile(xres[:, t, :], hT, t * 128)
            need_conv = b >= 7
            if is_s:
                segs = [(0, 64, 0), (64, 64, 1)]
            else:
                segs = [(0, 512, None)]
            for ci in range(6):
                w = 512 if ci < 5 else 256
                wu = w_get()
                wg_ = w_get()
                if need_conv:
                    bk = nb()
                    for kc in range(8):
                        P.mm(bk[:, :w], hT[:, kc, n - 128:n], wu[:, kc, 0:w], start=(kc == 0), stop=(kc == 7))
                    P.copy("dve", utok[:, ci * 512:ci * 512 + w], bk[:, :w])
                for fl in range(w // 128):
                    fo = ci * 4 + fl
                    c = slice(fl * 128, fl * 128 + 128)
                    bu = nb()
                    for kc in range(8):
                        P.mm(bu[:, :n], wu[:, kc, c], hT[:, kc, 0:n], start=(kc == 0), stop=(kc == 7))
                    bg = nb()
                    for kc in range(8):
                        P.mm(bg[:, :n], wg_[:, kc, c], hT[:, kc, 0:n], start=(kc == 0), stop=(kc == 7))
                    uo = 0
                    for (sc0, sl, ss) in segs:
                        ub = ubuf[:, uo:uo + sl + 2]
                        uo += sl + 2
                        halo = hal[:, fo, :] if ss is None else cch[ss][:, fo, :]
                        P.copy("pool", ub[:, 0:2], halo)
                        P.copy("act", ub[:, 2:2 + sl], bu[:, sc0:sc0 + sl])
                        cv = cva[:, sc0:sc0 + sl]
                        P.ts("dve", cv, ub[:, 0:sl], convp_s[:, fo, 0:1], convp_s[:, fo, 3:4], ALU.mult, ALU.add)
                        P.stt("dve", cv, ub[:, 1:1 + sl], convp_s[:, fo, 1:2], cv, ALU.mult, ALU.add)
                        P.stt("dve", cv, ub[:, 2:2 + sl], convp_s[:, fo, 2:3], cv, ALU.mult, ALU.add)
                        if ss is None and b < NBLK - 1:
                            P.copy("pool", hal[:, fo, :], ub[:, sl:sl + 2])
                    P.act(cvg[:, :n], cva[:, :n], AF.Gelu_apprx_tanh)
                    P.tt("dve", zT[:, fo, 0:n], cvg[:, :n], bg[:, :n], ALU.mult)
            if need_conv:
                if is_s:
                    for s in range(2):
                        P.dma("sp", oconv_s[2 * s:2 * s + 2, :], utok[64 * s + 62:64 * s + 64, :])
                else:
                    P.dma("sp", oconv_p.ap(), utok[126:128, :])
            for cg in range(2):
                banks = [nb() for _ in range(nt)]
                for kg in range(3):
                    wd = w_get()
                    nkc = 8 if kg < 2 else 6
                    for t in range(nt):
                        for kl in range(nkc):
                            kc = kg * 8 + kl
                            P.mm(banks[t][:, :], zT[:, kc, t * 128:(t + 1) * 128], wd[:, kl, :],
                                 start=(kc == 0), stop=(kc == NFC - 1))
                for t in range(nt):
                    xs_ = xres[:, t, cg * 512:(cg + 1) * 512]
                    P.stt("dve", xs_, xs_, ALPHA, banks[t][:], ALU.mult, ALU.add)
                    if cg == 1:
                        ln_stats(xres[:, t, :], t)
            ln_rstd(nt)
            for t in range(nt):
                ln_apply(xres[:, t, :], 2, t)
            P.dma("pool", ydst.rearrange("(t p) d -> p t d", p=128), xres[:, 0:nt, :])

        P.emit()
        build_nc.stats = (len(P.ops), P.n_waits)
    return nc


def _t5_bucket(rel):
    nb_, me = 16, 8
    ret = np.where(rel > 0, nb_, 0)
    n = np.abs(rel)
    nf = np.maximum(n, 1).astype(np.float32)
    large = me + (np.log(nf / np.float32(me)) / np.float32(math.log(128 / me)) * np.float32(nb_ - me)).astype(np.int32)
    large = np.minimum(large, nb_ - 1)
    return ret + np.where(n < me, n, large)


_NC_CACHE = {}


def kernel(**inputs):
    f32 = np.float32
    g = {k: np.asarray(v) for k, v in inputs.items()}
    p = np.arange(128)[:, None]
    mcol = np.arange(256)[None, :]
    rel = p - mcol
    ga = g["t5_table"][_t5_bucket(rel)]
    gb = g["rel_table_b"][0][np.clip(rel, -128, 128) + 128]
    gtab = np.ascontiguousarray(np.concatenate([ga, gb], axis=2).transpose(0, 2, 1)).astype(f32)
    ca = g["t5_table"][_t5_bucket(np.array([-1000]))][0]
    cb = g["rel_table_b"][0][0]
    ctab = np.ascontiguousarray(np.broadcast_to(np.concatenate([ca, cb])[None, :], (128, 24))).astype(f32)
    lamv = np.stack([g["lambda_q1"][0], g["lambda_k1"][0], g["lambda_q2"][0], g["lambda_k2"][0]], 0)
    lamv = np.ascontiguousarray(np.broadcast_to(lamv[None], (128, 4, 64))).astype(f32)
    sublng = np.ascontiguousarray(g["subln_g"][0].reshape(128, 1)).astype(f32)
    lntab = np.stack([g["ln1_g"][0], g["ln1_b"][0], g["ln2_g"][0], g["ln2_b"][0]], 0)
    lntab = np.ascontiguousarray(np.broadcast_to(lntab[None], (128, 4, D))).astype(f32)
    cvp = np.concatenate([g["conv_w"][0], g["conv_b"][0][None]], 0)
    convp = np.ascontiguousarray(cvp.reshape(4, NFC, 128).transpose(2, 1, 0)).astype(f32)
    idn = np.eye(128, dtype=f32)
    shared = {
        "w_in": np.ascontiguousarray(g["w_in"][0]), "w_pa": np.ascontiguousarray(g["w_pa"][0]),
        "w_pb": np.ascontiguousarray(g["w_pb"][0]), "w_out": np.ascontiguousarray(g["w_out"][0]),
        "w_up": np.ascontiguousarray(g["w_up"][0]), "w_gate": np.ascontiguousarray(g["w_gate"][0]),
        "w_down": np.ascontiguousarray(g["w_down"][0]),
        "lamv": lamv, "sublng": sublng, "lntab": lntab, "convp": convp, "gtab": gtab, "ctab": ctab, "idn": idn,
    }
    in_maps = []
    for c in range(N_CORES):
        sl = slice(2 * c, 2 * c + 2)
        m = dict(shared)
        m["xp"] = np.ascontiguousarray(g["x_prompt"][c])
        m["xs"] = np.ascontiguousarray(g["x_sample"][sl].reshape(2 * TS, D))
        m["cak"] = np.ascontiguousarray(g["cache_a_k"][0, sl].reshape(2, 2048, D))
        m["cav"] = np.ascontiguousarray(g["cache_a_v"][0, sl].reshape(2, 2048, D))
        m["cbk"] = np.ascontiguousarray(g["cache_b_k"][0, sl].reshape(2, 512, D))
        m["cbv"] = np.ascontiguousarray(g["cache_b_v"][0, sl].reshape(2, 512, D))
        m["cconv"] = np.ascontiguousarray(g["cache_conv"][0, sl])
        in_maps.append(m)
    if "nc" not in _NC_CACHE:
        _NC_CACHE["nc"] = build_nc()
    nc = _NC_CACHE["nc"]
    res = run_bass_kernel_spmd(nc, in_maps, core_ids=list(range(N_CORES)))
    R = res.results

    def cat(name):
        return np.stack([np.asarray(R[c][name]) for c in range(N_CORES)], 0)

    y_p = cat("y_p")
    y_s = cat("y_s").reshape(16, TS, D)
    oak_p = cat("oak_p").reshape(1, 8, S, 8, 2, 64)
    oav_p = cat("oav_p").reshape(1, 8, S, 8, 128)
    obk_p = cat("obk_p").reshape(1, 8, 512, 16, 64)
    obv_p = cat("obv_p").reshape(1, 8, 512, 16, 64)
    oconv_p = cat("oconv_p").reshape(1, 8, 2, DFF)
    oak_s = cat("oak_s").reshape(1, 16, TS, 8, 2, 64)
    oav_s = cat("oav_s").reshape(1, 16, TS, 8, 128)
    obk_s = cat("obk_s").reshape(1, 16, TS, 16, 64)
    obv_s = cat("obv_s").reshape(1, 16, TS, 16, 64)
    oconv_s = cat("oconv_s").reshape(1, 16, 2, DFF)
    outs = (y_p, y_s, oak_p, oav_p, obk_p, obv_p, oconv_p, oak_s, oav_s, obk_s, obv_s, oconv_s)
    return tuple(np.ascontiguousarray(o, dtype=f32) for o in outs)
```

```python
import math
import contextlib
import numpy as np
import concourse.bass as bass
import concourse.mybir as mybir
from concourse.bass_utils import run_bass_kernel_spmd

F32 = mybir.dt.float32
BF16 = mybir.dt.bfloat16
AF = mybir.ActivationFunctionType
ALU = mybir.AluOpType
AX = mybir.AxisListType

N_CORES = 8
D = 1024
S = 4096
NBLK = 8
TS = 64
DFF = 2816
NFC = 22
ALPHA = 2.0 ** 0.25
LAM_INIT = 0.2
EPS = 1e-5
NTOK = S + 2 * TS

COMPUTE = ("pe", "act", "dve", "pool")
N_DMA_SLOTS = {"sp": 12, "pool": 6}


def ap_box(ap):
    t = ap.tensor
    name = t.name
    pat = [list(x) for x in ap.ap]
    off = int(ap.offset)
    if "DRam" in type(t).__name__:
        lo = hi = off
        for st, n in pat:
            if n > 1:
                if st >= 0:
                    hi += st * (n - 1)
                else:
                    lo += st * (n - 1)
        return (name, 0, 0, lo, hi)
    fsz = 1
    for s_ in list(t.shape)[1:]:
        fsz *= s_
    pstep, pn = pat[0]
    p0 = off // fsz
    foff = off - p0 * fsz
    ps = pstep // fsz
    lo = hi = foff
    for st, n in pat[1:]:
        if n > 1:
            if st >= 0:
                hi += st * (n - 1)
            else:
                lo += st * (n - 1)
    return (name, p0, p0 + (pn - 1) * ps, lo, hi)


def boxes_overlap(a, b):
    return not (a[2] < b[1] or b[2] < a[1] or a[4] < b[3] or b[4] < a[3])


def box_contains(a, b):
    return a[1] <= b[1] and a[2] >= b[2] and a[3] <= b[3] and a[4] >= b[4]


class Op:
    __slots__ = ("eng", "fn", "reads", "writes", "dma", "idx", "deps", "sig", "slot", "slotval", "signals")

    def __init__(self, eng, fn, reads, writes, dma):
        self.eng = eng
        self.fn = fn
        self.reads = reads
        self.writes = writes
        self.dma = dma
        self.deps = set()
        self.sig = None
        self.slot = None
        self.slotval = None
        self.signals = False


class Prog:
    def __init__(self, nc):
        self.nc = nc
        self.ops = []
        self.hist = {}

    def op(self, eng, fn, reads=(), writes=(), dma=False):
        rb, wb = [], []
        for a in reads:
            if "PSum" in type(a.tensor).__name__:
                wb.append((a.tensor.name, 0, 127, 0, 1 << 30))
            else:
                rb.append(ap_box(a))
        for a in writes:
            if "PSum" in type(a.tensor).__name__:
                wb.append((a.tensor.name, 0, 127, 0, 1 << 30))
            else:
                wb.append(ap_box(a))
        o = Op(eng, fn, rb, wb, dma)
        o.idx = len(self.ops)
        self.ops.append(o)
        ops = self.ops
        for b in o.reads:
            for (hb, hi, hw) in self.hist.setdefault(b[0], []):
                if hw and boxes_overlap(hb, b):
                    o.deps.add(hi)
        for b in o.writes:
            for (hb, hi, hw) in self.hist.setdefault(b[0], []):
                if boxes_overlap(hb, b):
                    o.deps.add(hi)
        o.deps.discard(o.idx)
        for b in o.writes:
            h = self.hist[b[0]]
            h[:] = [e for e in h if not box_contains(b, e[0])]
            h.append((b, o.idx, True))
        for b in o.reads:
            h = self.hist[b[0]]
            if not o.dma:
                h[:] = [e for e in h if not ((not e[2]) and e[0] == b and ops[e[1]].eng == o.eng
                                             and not ops[e[1]].dma)]
            h.append((b, o.idx, False))
        return o

    def emit(self):
        nc = self.nc
        ops = self.ops
        for o in ops:
            for d in list(o.deps):
                p = ops[d]
                if (not p.dma) and (not o.dma) and p.eng == o.eng and p.eng == "pe":
                    o.deps.discard(d)
                    continue
                p.signals = True
        cnt = {e: 0 for e in COMPUTE}
        slot_rr = {q: 0 for q in N_DMA_SLOTS}
        slot_cnt = {q: [0] * n for q, n in N_DMA_SLOTS.items()}
        for o in ops:
            if o.dma:
                q = o.eng
                s = slot_rr[q]
                slot_rr[q] = (s + 1) % N_DMA_SLOTS[q]
                slot_cnt[q][s] += 16
                o.slot = (q, s)
                o.slotval = slot_cnt[q][s]
            elif o.signals:
                cnt[o.eng] += 1
                o.sig = cnt[o.eng]
        with contextlib.ExitStack() as es:
            sems = {e: es.enter_context(nc.semaphore("s_" + e)) for e in COMPUTE}
            dsems = {q: [es.enter_context(nc.semaphore("d_%s%d" % (q, i))) for i in range(n)]
                     for q, n in N_DMA_SLOTS.items()}
            block = es.enter_context(nc.Block())
            by_eng = {}
            for o in ops:
                by_eng.setdefault(o.eng, []).append(o)
            self.n_waits = 0

            def run_engine(ename, eng):
                waited = {}

                def wait(key, sem, val):
                    if waited.get(key, 0) >= val:
                        return
                    eng.wait_ge(sem, val)
                    self.n_waits += 1
                    waited[key] = val

                for o in by_eng.get(ename, []):
                    need = {}
                    for d in o.deps:
                        p = ops[d]
                        if p.dma:
                            k = ("d",) + p.slot
                            need[k] = max(need.get(k, 0), p.slotval)
                        else:
                            k = ("c", p.eng)
                            need[k] = max(need.get(k, 0), p.sig)
                    if o.dma:
                        k = ("d",) + o.slot
                        need[k] = max(need.get(k, 0), o.slotval - 16)
                    for k, v in sorted(need.items()):
                        if v <= 0:
                            continue
                        if k[0] == "c":
                            wait(k, sems[k[1]], v)
                        else:
                            wait(k, dsems[k[1]][k[2]], v)
                    ins = o.fn(eng)
                    if o.dma:
                        ins.then_inc(dsems[o.slot[0]][o.slot[1]], 16)
                    elif o.signals:
                        ins.then_inc(sems[o.eng], 1)
                if ename == "sp":
                    for q, n in N_DMA_SLOTS.items():
                        for s in range(n):
                            if slot_cnt[q][s] > 0:
                                wait(("d", q, s), dsems[q][s], slot_cnt[q][s])

            @block.tensor
            def _(e):
                run_engine("pe", e)

            @block.scalar
            def _(e):
                run_engine("act", e)

            @block.vector
            def _(e):
                run_engine("dve", e)

            @block.gpsimd
            def _(e):
                run_engine("pool", e)

            @block.sync
            def _(e):
                run_engine("sp", e)

    def dma(self, q, out, in_):
        return self.op(q, lambda e: e.dma_start(out=out, in_=in_), reads=[in_], writes=[out], dma=True)

    def mm(self, out, lhsT, rhs, start=True, stop=True):
        return self.op("pe", lambda e: e.matmul(out, lhsT, rhs, start=start, stop=stop),
                       reads=[lhsT, rhs], writes=[out])

    def transpose(self, out, in_, ident):
        return self.op("pe", lambda e: e.transpose(out, in_, ident), reads=[in_, ident], writes=[out])

    def act(self, out, in_, func, bias=None, scale=1.0):
        kw = {}
        rd = [in_]
        if bias is not None:
            kw["bias"] = bias
            if not isinstance(bias, (int, float)):
                rd.append(bias)
        return self.op("act", lambda e: e.activation(out, in_, func, scale=scale, **kw), reads=rd, writes=[out])

    def tt(self, eng, out, in0, in1, op):
        return self.op(eng, lambda e: e.tensor_tensor(out, in0, in1, op), reads=[in0, in1], writes=[out])

    def ts(self, eng, out, in0, s1, s2, op0, op1=None):
        rd = [in0] + [s for s in (s1, s2) if s is not None and not isinstance(s, (int, float))]
        if op1 is None:
            return self.op(eng, lambda e: e.tensor_scalar(out, in0, s1, s2, op0), reads=rd, writes=[out])
        return self.op(eng, lambda e: e.tensor_scalar(out, in0, s1, s2, op0, op1), reads=rd, writes=[out])

    def stt(self, eng, out, in0, scalar, in1, op0, op1):
        rd = [in0, in1] + ([] if isinstance(scalar, (int, float)) else [scalar])
        return self.op(eng, lambda e: e.scalar_tensor_tensor(out, in0, scalar, in1, op0, op1), reads=rd, writes=[out])

    def copy(self, eng, out, in_):
        if eng == "act":
            return self.op(eng, lambda e: e.copy(out, in_), reads=[in_], writes=[out])
        return self.op(eng, lambda e: e.tensor_copy(out, in_), reads=[in_], writes=[out])

    def memset(self, eng, out, val):
        return self.op(eng, lambda e: e.memset(out, val), reads=[], writes=[out])

    def recip(self, out, in_):
        return self.op("dve", lambda e: e.reciprocal(out, in_), reads=[in_], writes=[out])

    def recip_fast(self, out, in_):
        return self.op("dve", lambda e: e.reciprocal_approx_fast(out, in_), reads=[in_], writes=[out])


def carve(T, off, shape):
    n = 1
    for s_ in shape[1:]:
        n *= s_
    ap = T[:, off:off + n]
    if len(shape) == 3:
        ap = ap.rearrange("p (a b) -> p a b", a=shape[1])
    return ap


def v3(ap, a):
    return ap.rearrange("p (a b) -> p a b", a=a)


def build_nc(n_units=16, n_cblocks=9):
    nc = bass.Bass("TRN2", target_bir_lowering=False)

    def di(n, s):
        return nc.dram_tensor(n, s, F32, kind="ExternalInput")

    def do(n, s):
        return nc.dram_tensor(n, s, F32, kind="ExternalOutput")

    xp = di("xp", [S, D])
    xs = di("xs", [2 * TS, D])
    cak = di("cak", [2, 2048, D])
    cav = di("cav", [2, 2048, D])
    cbk = di("cbk", [2, 512, D])
    cbv = di("cbv", [2, 512, D])
    cconv = di("cconv", [2, 2, DFF])
    w_in = di("w_in", [D, 8192])
    w_pa = di("w_pa", [D, D])
    w_pb = di("w_pb", [D, D])
    w_out = di("w_out", [D, D])
    w_up = di("w_up", [D, DFF])
    w_gate = di("w_gate", [D, DFF])
    w_down = di("w_down", [DFF, D])
    lamv = di("lamv", [128, 4, 64])
    sublng = di("sublng", [128, 1])
    lntab = di("lntab", [128, 4, D])
    convp = di("convp", [128, NFC, 4])
    gtab = di("gtab", [128, 24, 256])
    ctab_d = di("ctab", [128, 24])
    idn = di("idn", [128, 128])

    y_p = do("y_p", [S, D])
    y_s = do("y_s", [2 * TS, D])
    oak_p = do("oak_p", [S, D])
    oav_p = do("oav_p", [S, D])
    obk_p = do("obk_p", [512, D])
    obv_p = do("obv_p", [512, D])
    oconv_p = do("oconv_p", [2, DFF])
    oak_s = do("oak_s", [2 * TS, D])
    oav_s = do("oav_s", [2 * TS, D])
    obk_s = do("obk_s", [2 * TS, D])
    obv_s = do("obv_s", [2 * TS, D])
    oconv_s = do("oconv_s", [4, DFF])

    ond = nc.dram_tensor("ond", [16, 128, NTOK], BF16)
    wg_b = nc.dram_tensor("wg_b", [D, 2048], BF16)
    wpa_b = nc.dram_tensor("wpa_b", [D, D], BF16)
    wpb_b = nc.dram_tensor("wpb_b", [D, D], BF16)
    wo_b = nc.dram_tensor("wo_b", [D, D], BF16)
    wup_b = nc.dram_tensor("wup_b", [D, DFF], BF16)
    wgt_b = nc.dram_tensor("wgt_b", [D, DFF], BF16)
    wdn_b = nc.dram_tensor("wdn_b", [DFF, D], BF16)

    with contextlib.ExitStack() as es:
        def sb(n, s, d):
            return es.enter_context(nc.sbuf_tensor(n, s, d))

        AR = sb("AR", [128, 65536], BF16)
        FA = sb("FA", [128, 14336], F32)
        ET = [sb("ET%d" % i, [128, 512], BF16) for i in range(8)]
        ident = sb("ident", [128, 128], F32)
        onesb = sb("onesb", [128, 128], BF16)
        onesf = sb("onesf", [128, 128], F32)
        epsb = sb("epsb", [128, 1], F32)
        lamt = sb("lamt", [128, 8], F32)
        ctab = sb("ctab_s", [128, 24], F32)
        lamv_s = sb("lamv_s", [128, 4, 64], F32)
        convp_s = sb("convp_s", [128, NFC, 4], F32)
        hal = sb("hal", [128, NFC, 2], F32)
        cch = [sb("cch%d" % s, [128, NFC, 2], F32) for s in range(2)]
        bst = sb("bst", [128, 4, 12], F32)
        mv = sb("mv", [128, 4, 2], F32)
        rstd = sb("rstd", [128, 4], F32)
        BK = [es.enter_context(nc.psum_tensor("BK%d" % i, [128, 512], F32)) for i in range(8)]

        P = Prog(nc)
        rows = [slice(0, 64), slice(64, 128)]

        xT = carve(AR, 0, [128, 8, NTOK])
        QT = carve(AR, 33792, [128, NTOK])
        KT = carve(AR, 38016, [128, NTOK])
        Vt = carve(AR, 42240, [128, 32, 128])
        Vs = carve(AR, 46336, [128, 2, 128])
        OnT = carve(AR, 46592, [128, NTOK])
        Wh = [carve(AR, 50816, [128, 8, 384]), carve(AR, 53888, [128, 8, 384])]
        ckT = [carve(AR, 56960, [128, 2048]), carve(AR, 59008, [128, 2048])]
        cV = [carve(AR, 61056, [128, 16, 128]), carve(AR, 63104, [128, 16, 128])]
        gt = [carve(FA, 0, [128, 2, 256]), carve(FA, 512, [128, 2, 256])]
        kvst = [carve(FA, 1024, [128, 4, 256]), carve(FA, 2048, [128, 4, 256])]
        tmpn = [carve(FA, 3072, [128, 512]), carve(FA, 3584, [128, 512]), carve(FA, 10304, [128, 512]),
                carve(FA, 11840, [128, 512])]
        fb = [carve(FA, 4096 + 512 * i, [128, 512]) for i in range(10)]
        ckf = [carve(FA, 9216, [128, 4, 128]), carve(FA, 9728, [128, 4, 128])]
        lprod = carve(FA, 10240, [128, 64])
        fbo = [carve(FA, 10816, [128, 512]), carve(FA, 11328, [128, 512])]

        STB = [BK[0], BK[1], BK[6], BK[7]]
        OTB = [BK[2], BK[3]]
        SMB = [BK[4], BK[5]]
        PJ = [BK[7], BK[0], BK[1], BK[6]]
        cnt = {"pj": 0, "item": 0, "kv": 0, "ckf": 0, "nb": 0, "fbo": 0}

        def pj():
            cnt["pj"] += 1
            return PJ[cnt["pj"] % 4]

        P.dma("sp", ident[:], idn.ap())
        P.dma("sp", lamv_s[:], lamv.ap())
        P.dma("sp", ctab[:], ctab_d.ap())
        P.dma("sp", convp_s[:], convp.ap())
        P.dma("sp", lamt[:, 5:6], sublng.ap())
        P.memset("dve", onesb[:], 1.0)
        P.memset("dve", onesf[:], 1.0)
        P.memset("dve", epsb[:], EPS)
        P.memset("dve", hal[:], 0.0)
        for i in range(2):
            P.tt("dve", lprod, lamv_s[:, 2 * i, :], lamv_s[:, 2 * i + 1, :], ALU.mult)
            P.op("dve", lambda e, i=i: e.reduce_sum(lamt[:, i:i + 1], lprod, AX.X), reads=[lprod],
                 writes=[lamt[:, i:i + 1]])
        P.act(lamt[:, 2:4], lamt[:, 0:2], AF.Exp)
        P.tt("dve", lamt[:, 4:5], lamt[:, 3:4], lamt[:, 2:3], ALU.subtract)
        P.ts("dve", lamt[:, 4:5], lamt[:, 4:5], -LAM_INIT, None, ALU.add)
        P.ts("dve", lamt[:, 5:6], lamt[:, 5:6], 1.0 - LAM_INIT, None, ALU.mult)
        neglam = lamt[:, 4:5]
        gsub = lamt[:, 5:6]

        def load_unit_weights(u):
            isA = u < 8
            qc0 = (0 if isA else 3072) + (u % 8) * 128
            W = Wh[u % 2]
            for i, c0 in enumerate((qc0, qc0 + 1024, qc0 + 2048)):
                P.dma("pool", W[:, :, i * 128:(i + 1) * 128],
                      w_in[:, c0:c0 + 128].rearrange("(kc p) n -> p kc n", p=128))
            if isA:
                P.dma("sp", gt[u % 2][:, 0:1, :], gtab[:, u:u + 1, :])
            else:
                j = u - 8
                P.dma("sp", gt[u % 2][:, 0:2, :], gtab[:, 8 + 2 * j:10 + 2 * j, :])

        load_unit_weights(0)

        wcasts = []
        for r0 in range(0, D, 256):
            wcasts.append((wg_b[r0:r0 + 256, :], w_in[r0:r0 + 256, 6144:8192]))
            wcasts.append((wpa_b[r0:r0 + 256, :], w_pa[r0:r0 + 256, :]))
            wcasts.append((wpb_b[r0:r0 + 256, :], w_pb[r0:r0 + 256, :]))
            wcasts.append((wo_b[r0:r0 + 256, :], w_out[r0:r0 + 256, :]))
            wcasts.append((wup_b[r0:r0 + 256, :], w_up[r0:r0 + 256, :]))
            wcasts.append((wgt_b[r0:r0 + 256, :], w_gate[r0:r0 + 256, :]))
        for r0 in range(0, DFF, 256):
            wcasts.append((wdn_b[r0:r0 + 256, :], w_down[r0:r0 + 256, :]))
        wc_i = [0]

        def emit_wcasts(k):
            for _ in range(k):
                if wc_i[0] < len(wcasts):
                    d_, s_ = wcasts[wc_i[0]]
                    P.dma("pool", d_, s_)
                    wc_i[0] += 1

        def transpose_tile(src, dstT, col0, width=128):
            for g in range(2):
                bank = BK[cnt["nb"] % 8]
                cnt["nb"] += 1
                for i in range(4):
                    kc = g * 4 + i
                    P.transpose(bank[:, i * 128:(i + 1) * 128], src[:, kc * 128:(kc + 1) * 128], ident[:])
                P.copy("act" if g == 0 else "dve", dstT[:, 4 * g:4 * g + 4, col0:col0 + 128], v3(bank[:], 4))

        for t in range(33):
            stage = kvst[t % 2].rearrange("p a b -> p (a b)")
            src = xp[t * 128:(t + 1) * 128, :] if t < 32 else xs.ap()
            P.dma("sp", stage, src)
            transpose_tile(stage, xT, t * 128)

        def attn(tiles, qcol0, tsel, csel):
            pend = []
            DEFER_AT = 4
            ntl = len(tiles)

            def flush_pair(pr):
                T, ets, ti = pr
                nk, a, b = T["nk"], T["c_lo"], T["c_hi"]
                first = ti == 0
                last = ti == ntl - 1
                for m in range(2):
                    P.mm(OTB[m][:, a:b], T["V"], ets[m][0:nk, a:b], start=first, stop=last)
                for m in range(2):
                    P.mm(SMB[m][:, a:b], onesb[0:nk, :], ets[m][0:nk, a:b], start=first, stop=last)

            for ti, T in enumerate(tiles):
                nk, a, b = T["nk"], T["c_lo"], T["c_hi"]
                qa = qcol0 + (a - T["cbase"])
                bufs = []
                for m in range(2):
                    k = cnt["item"]
                    cnt["item"] += 1
                    st = STB[k % 4]
                    bufs.append((st, ET[k % 8], tmpn[k % 4]))
                    P.mm(st[0:nk, a:b], T["kT"](m), QT[rows[m], qa:qa + (b - a)])
                ets = []
                for m in range(2):
                    st, et, tm = bufs[m]
                    for (pa, pb, kind, tc0) in T["parts"]:
                        if kind == "far":
                            P.act(et[0:nk, pa:pb], st[0:nk, pa:pb], AF.Exp,
                                  bias=ctab[0:nk, csel(m):csel(m) + 1], scale=0.125)
                        else:
                            P.stt("dve", tm[0:nk, pa:pb], st[0:nk, pa:pb], 0.125,
                                  T["gt"][0:nk, tsel(m), tc0:tc0 + (pb - pa)], ALU.mult, ALU.add)
                            P.act(et[0:nk, pa:pb], tm[0:nk, pa:pb], AF.Exp)
                    if T["corner"] is not None:
                        r0, r1, c0, c1 = T["corner"]
                        P.memset("dve", et[r0:r1, c0:c1], 0.0)
                    ets.append(et)
                pend.append((T, ets, ti))
                if len(pend) > 1:
                    flush_pair(pend.pop(0))
                if ti == DEFER_AT:
                    run_deferred()
            while pend:
                flush_pair(pend.pop(0))
            run_deferred()

        deferred = []

        def run_deferred():
            while deferred:
                deferred.pop(0)()

        def finalize(isA, c0, n, dst):
            cs = slice(c0, c0 + n)
            if isA:
                fo_ = fbo[cnt["fbo"] % 2]
                cnt["fbo"] += 1
                P.copy("act", fb[6][:, :n], SMB[0][:, cs])
                P.copy("dve", fb[7][:, :n], OTB[0][:, cs])
                P.copy("act", fb[8][:, :n], SMB[1][:, cs])
                P.copy("dve", fb[9][:, :n], OTB[1][:, cs])
                P.recip(fb[0][:, :n], fb[6][:, :n])
                P.tt("dve", fb[1][:, :n], fb[7][:, :n], fb[0][:, :n], ALU.mult)
                P.recip(fb[2][:, :n], fb[8][:, :n])
                P.tt("dve", fb[3][:, :n], fb[9][:, :n], fb[2][:, :n], ALU.mult)
                P.stt("dve", fo_[:, :n], fb[3][:, :n], neglam, fb[1][:, :n], ALU.mult, ALU.add)
                P.tt("dve", fb[5][:, :n], fo_[:, :n], fo_[:, :n], ALU.mult)

                def tail(fo_=fo_, n=n, dst=dst):
                    bank = BK[7]
                    P.mm(bank[:, :n], onesf[:], fb[5][:, :n])
                    P.act(fb[4][:, :n], bank[:, :n], AF.Sqrt, bias=epsb[:], scale=1.0 / 128)
                    P.recip(fb[4][:, :n], fb[4][:, :n])
                    P.stt("dve", dst, fo_[:, :n], gsub, fb[4][:, :n], ALU.mult, ALU.mult)
                deferred.append(tail)
            else:
                for m in range(2):
                    r = rows[m]
                    P.copy("act", fb[6][r, :n], SMB[m][r, cs])
                    P.copy("dve", fb[7][r, :n], OTB[m][r, cs])
                P.recip(fb[0][:, :n], fb[6][:, :n])
                P.tt("dve", dst, fb[7][:, :n], fb[0][:, :n], ALU.mult)

        def prompt_tiles(isA, qb, g):
            def mk(kt, c_lo, c_hi, parts, corner):
                return dict(kT=lambda m, kt=kt: KT[rows[m], kt * 128:(kt + 1) * 128], V=Vt[:, kt, :], nk=128,
                            c_lo=c_lo, c_hi=c_hi, cbase=0, parts=parts, corner=corner, gt=g)

            def diag(i):
                c0 = 128 * i
                parts = [(c0, min(c0 + 256, 512), "near", 0)]
                if c0 + 256 < 512:
                    parts.append((c0 + 256, 512, "far", 0))
                return mk(4 * qb + i, c0, 512, parts, (64, 128, c0, c0 + 64))

            tiles = []
            if isA:
                for kt in range(0, 4 * qb - 1):
                    tiles.append(mk(kt, 0, 512, [(0, 512, "far", 0)], None))
                if qb > 0:
                    tiles.append(mk(4 * qb - 1, 0, 512, [(0, 128, "near", 128), (128, 512, "far", 0)], None))
                for i in range(4):
                    tiles.append(diag(i))
            else:
                tiles.append(diag(0))
                if qb > 0:
                    for ip in range(4):
                        c_hi = 128 * (ip + 1)
                        corner = (0, 64, 128 * ip + 64, 128 * ip + 128)
                        if ip < 3:
                            parts = [(0, c_hi, "far", 0)]
                        else:
                            parts = [(0, 128, "near", 128), (128, 512, "far", 0)]
                        tiles.append(mk(4 * qb - 4 + ip, 0, c_hi, parts, corner))
                for i in range(1, 4):
                    tiles.append(diag(i))
            return tiles

        def sample_tiles(isA, s, g):
            ncache = 16 if isA else 4
            cb = 64 * s
            tiles = []
            for t in range(ncache):
                parts = [(cb, cb + 64, "far", 0)] if t < ncache - 1 else [(cb, cb + 64, "near", 128)]
                tiles.append(dict(kT=lambda m, t=t: ckT[s][rows[m], t * 128:(t + 1) * 128], V=cV[s][:, t, :],
                                  nk=128, c_lo=cb, c_hi=cb + 64, cbase=cb, parts=parts, corner=None, gt=g))
            tiles.append(dict(kT=lambda m: KT[rows[m], S + 64 * s:S + 64 * s + 64], V=Vs[0:64, s, :], nk=64,
                              c_lo=cb, c_hi=cb + 64, cbase=cb, parts=[(cb, cb + 64, "near", 0)], corner=None,
                              gt=g))
            return tiles

        n_wc_per_unit = (len(wcasts) + 15) // 16
        for u in range(n_units):
            isA = u < 8
            hj = u % 8
            W = Wh[u % 2]
            g = gt[u % 2]
            if u + 1 < n_units:
                load_unit_weights(u + 1)
            emit_wcasts(n_wc_per_unit)
            colsK = slice(hj * 128, hj * 128 + 128)
            if isA:
                tsel = lambda m: 0
                csel = lambda m, hj=hj: hj
                okd, ovd, oks, ovs = oak_p, oav_p, oak_s, oav_s
                ck_src, cv_src, ncch = cak, cav, 4
            else:
                tsel = lambda m: m
                csel = lambda m, hj=hj: 8 + 2 * hj + m
                okd, ovd, oks, ovs = obk_p, obv_p, obk_s, obv_s
                ck_src, cv_src, ncch = cbk, cbv, 1
            for s in range(2):
                P.dma("pool", cV[s][:, 0:4 * ncch, :],
                      cv_src[s, :, colsK].rearrange("(t p) c -> p t c", p=128))
            for blk in range(NBLK):
                cols = slice(blk * 512, blk * 512 + 512)
                for wi, dst in ((0, QT), (1, KT)):
                    bank = pj()
                    for kc in range(8):
                        P.mm(bank[:, :], W[:, kc, wi * 128:(wi + 1) * 128], xT[:, kc, cols],
                             start=(kc == 0), stop=(kc == 7))
                    P.copy("act", dst[:, cols], bank[:])
                need_out = isA or blk == NBLK - 1
                if need_out:
                    st_ = kvst[cnt["kv"] % 2]
                    cnt["kv"] += 1
                    for half in range(2):
                        bank = pj()
                        for tt in range(2):
                            t = blk * 4 + half * 2 + tt
                            for kc in range(8):
                                P.mm(bank[:, tt * 256:(tt + 1) * 256], xT[:, kc, t * 128:(t + 1) * 128],
                                     W[:, kc, 128:384], start=(kc == 0), stop=(kc == 7))
                        P.copy("act", st_[:, half * 2:half * 2 + 2, :], v3(bank[:], 2))
                        P.copy("dve", Vt[:, blk * 4 + half * 2:blk * 4 + half * 2 + 2, :],
                               st_[:, half * 2:half * 2 + 2, 128:256])
                    r0 = blk * 512 if isA else 0
                    P.dma("sp", okd[r0:r0 + 512, colsK].rearrange("(t p) c -> p t c", p=128), st_[:, :, 0:128])
                    P.dma("sp", ovd[r0:r0 + 512, colsK].rearrange("(t p) c -> p t c", p=128), st_[:, :, 128:256])
                else:
                    bank = pj()
                    for tt in range(4):
                        t = blk * 4 + tt
                        for kc in range(8):
                            P.mm(bank[:, tt * 128:(tt + 1) * 128], xT[:, kc, t * 128:(t + 1) * 128],
                                 W[:, kc, 256:384], start=(kc == 0), stop=(kc == 7))
                    P.copy("dve", Vt[:, blk * 4:blk * 4 + 4, :], v3(bank[:], 4))
                attn(prompt_tiles(isA, blk, g), blk * 512, tsel, csel)
                finalize(isA, 0, 512, OnT[:, cols])
            cs_ = slice(S, S + 128)
            for wi, dst in ((0, QT), (1, KT)):
                bank = pj()
                for kc in range(8):
                    P.mm(bank[:, 0:128], W[:, kc, wi * 128:(wi + 1) * 128], xT[:, kc, cs_],
                         start=(kc == 0), stop=(kc == 7))
                P.copy("dve", dst[:, cs_], bank[:, 0:128])
            st_ = kvst[cnt["kv"] % 2]
            cnt["kv"] += 1
            bank = pj()
            for s in range(2):
                for kc in range(8):
                    P.mm(bank[0:64, s * 256:(s + 1) * 256], xT[:, kc, S + 64 * s:S + 64 * s + 64],
                         W[:, kc, 128:384], start=(kc == 0), stop=(kc == 7))
            P.copy("dve", st_[0:64, 0:2, :], v3(bank[0:64, :], 2))
            P.copy("dve", Vs[0:64, :, :], st_[0:64, 0:2, 128:256])
            P.dma("sp", oks[:, colsK].rearrange("(s t) c -> t s c", s=2), st_[0:64, 0:2, 0:128])
            P.dma("sp", ovs[:, colsK].rearrange("(s t) c -> t s c", s=2), st_[0:64, 0:2, 128:256])
            for s in range(2):
                for c4 in range(ncch):
                    cf = ckf[cnt["ckf"] % 2]
                    cnt["ckf"] += 1
                    P.dma("sp", cf[:], ck_src[s, c4 * 512:(c4 + 1) * 512, colsK].rearrange("(t p) c -> p t c", p=128))
                    bank = pj()
                    for i in range(4):
                        P.transpose(bank[:, i * 128:(i + 1) * 128], cf[:, i, :], ident[:])
                    P.copy("dve", ckT[s][:, c4 * 512:(c4 + 1) * 512], bank[:])
                attn(sample_tiles(isA, s, g), S + 64 * s, tsel, csel)
            finalize(isA, 0, 128, OnT[:, cs_])
            deferred.append(lambda u=u: P.dma("sp", ond[u], OnT))
        run_deferred()
        emit_wcasts(len(wcasts))

        NR = 7
        HELD = 4
        ring = [carve(AR, i * 4096, [128, 8, 512]) for i in range(NR)]
        zT = carve(AR, 28672, [128, NFC, 512])
        mT = carve(AR, 39936, [128, 8, 512])
        hT = carve(AR, 44032, [128, 8, 512])
        xTb = carve(AR, 48128, [128, 8, 512])
        onb = carve(AR, 52224, [128, 8, 512])
        obb = carve(AR, 56320, [128, 8, 512])
        xres = carve(FA, 0, [128, 4, D])
        lnt = carve(FA, 4096, [128, 4, D])
        sgA = carve(FA, 8192, [128, 512])
        sgB = carve(FA, 8704, [128, 512])
        m1 = carve(FA, 9216, [128, 512])
        ubuf = carve(FA, 9728, [128, 520])
        cva = carve(FA, 10248, [128, 512])
        cvg = carve(FA, 10760, [128, 512])
        utok = carve(FA, 11272, [128, DFF])

        def nb():
            b_ = BK[cnt["nb"] % 8]
            cnt["nb"] += 1
            return b_

        chunks = []
        for b in range(n_cblocks):
            for hf in range(2):
                c = slice(hf * 512, hf * 512 + 512)
                c2 = slice(1024 + hf * 512, 1024 + hf * 512 + 512)
                chunks += [(wg_b[:, c], 8, 512), (wpa_b[:, c], 8, 512), (wg_b[:, c2], 8, 512), (wpb_b[:, c], 8, 512)]
            chunks += [(wo_b[:, 0:512], 8, 512), (wo_b[:, 512:1024], 8, 512)]
            for ci in range(6):
                w = 512 if ci < 5 else 256
                chunks += [(wup_b[:, ci * 512:ci * 512 + w], 8, w), (wgt_b[:, ci * 512:ci * 512 + w], 8, w)]
            for cg in range(2):
                for kg in range(3):
                    nkc = 8 if kg < 2 else 6
                    chunks.append((wdn_b[kg * 1024:kg * 1024 + nkc * 128, cg * 512:(cg + 1) * 512], nkc, 512))
        wstate = {"issued": 0, "got": 0}

        def w_issue():
            k = wstate["issued"]
            if k >= len(chunks):
                return
            src, nkc, w = chunks[k]
            P.dma("sp", ring[k % NR][:, 0:nkc, 0:w], src.rearrange("(kc p) n -> p kc n", p=128))
            wstate["issued"] += 1

        def w_get():
            k = wstate["got"]
            wstate["got"] += 1
            while wstate["issued"] < min(len(chunks), k + NR - HELD + 1):
                w_issue()
            return ring[k % NR]

        if n_cblocks > 0:
            P.dma("sp", lnt, lntab.ap())
            ccs = FA[0:2, 11272:11272 + DFF]
            for s in range(2):
                P.dma("sp", ccs, cconv[s])
                bank = nb()
                for fo in range(NFC):
                    P.mm(bank[:, 2 * fo:2 * fo + 2], ccs[:, fo * 128:(fo + 1) * 128], ident[0:2, 0:2])
                P.copy("dve", cch[s][:], v3(bank[:, 0:2 * NFC], NFC))

        def ln_stats(xt, t):
            for c in range(2):
                P.op("dve", lambda e, c=c: e.bn_stats(bst[:, t, 6 * c:6 * c + 6], xt[:, c * 512:(c + 1) * 512]),
                     reads=[xt[:, c * 512:(c + 1) * 512]], writes=[bst[:, t, 6 * c:6 * c + 6]])
            P.op("dve", lambda e: e.bn_aggr(mv[:, t, :], bst[:, t, :]), reads=[bst[:, t, :]], writes=[mv[:, t, :]])

        def ln_rstd(nt):
            P.act(rstd[:, 0:nt], mv[:, 0:nt, 1], AF.Sqrt, bias=epsb[:], scale=1.0)
            P.recip(rstd[:, 0:nt], rstd[:, 0:nt])

        def ln_apply(xt, gi, t):
            P.ts("dve", xt, xt, mv[:, t, 0:1], rstd[:, t:t + 1], ALU.subtract, ALU.mult)
            P.tt("dve", xt, xt, lnt[:, gi, :], ALU.mult)
            P.tt("pool", xt, xt, lnt[:, gi + 1, :], ALU.add)

        stg = [carve(FA, 11272, [128, D]), carve(FA, 12296, [128, D])]

        def blk_info(b):
            is_s = b == 8
            n = 128 if is_s else 512
            c0 = S if is_s else b * 512
            xsrc = xs.ap() if is_s else xp[b * 512:(b + 1) * 512, :]
            return is_s, n, c0, xsrc

        def prefetch_x_dma(b, t):
            _, n, c0, xsrc = blk_info(b)
            P.dma("sp", stg[t % 2], xsrc[t * 128:(t + 1) * 128, :])

        def prefetch_inputs(b):
            _, n, c0, xsrc = blk_info(b)
            nt = n // 128
            P.dma("sp", onb[:, :, 0:n], ond[0:8, :, c0:c0 + n].rearrange("u p c -> p u c"))
            P.dma("sp", obb[:, :, 0:n], ond[8:16, :, c0:c0 + n].rearrange("u p c -> p u c"))
            for t in range(nt):
                if t >= 2:
                    prefetch_x_dma(b, t)
                transpose_tile(stg[t % 2], xTb, t * 128)

        if n_cblocks > 0:
            for t in range(2):
                prefetch_x_dma(0, t)
            prefetch_inputs(0)
        for b in range(n_cblocks):
            is_s, n, c0, xsrc = blk_info(b)
            nt = n // 128
            ydst = y_s.ap() if is_s else y_p[b * 512:(b + 1) * 512, :]
            P.dma("sp", xres[:, 0:nt, :], xsrc.rearrange("(t p) d -> p t d", p=128))
            if b + 1 < n_cblocks:
                for t in range(min(2, blk_info(b + 1)[1] // 128)):
                    prefetch_x_dma(b + 1, t)
            for hf in range(2):
                wga = w_get()
                wpa = w_get()
                wgb = w_get()
                wpb = w_get()
                for fl in range(4):
                    fo = hf * 4 + fl
                    c = slice(fl * 128, fl * 128 + 128)
                    bk = nb()
                    for kc in range(8):
                        P.mm(bk[:, :n], wga[:, kc, c], xTb[:, kc, 0:n], start=(kc == 0), stop=(kc == 7))
                    P.act(sgA[:, :n], bk[:, :n], AF.Sigmoid)
                    bk = nb()
                    for kc in range(8):
                        P.mm(bk[:, :n], wpa[:, kc, c], onb[:, kc, 0:n], start=(kc == 0), stop=(kc == 7))
                    P.tt("dve", m1[:, :n], bk[:, :n], sgA[:, :n], ALU.mult)
                    bk = nb()
                    for kc in range(8):
                        P.mm(bk[:, :n], wgb[:, kc, c], xTb[:, kc, 0:n], start=(kc == 0), stop=(kc == 7))
                    P.act(sgB[:, :n], bk[:, :n], AF.Sigmoid)
                    bk = nb()
                    for kc in range(8):
                        P.mm(bk[:, :n], wpb[:, kc, c], obb[:, kc, 0:n], start=(kc == 0), stop=(kc == 7))
                    P.tt("dve", sgB[:, :n], bk[:, :n], sgB[:, :n], ALU.mult)
                    P.tt("pool", mT[:, fo, 0:n], m1[:, :n], sgB[:, :n], ALU.add)
            if b + 1 < n_cblocks:
                prefetch_inputs(b + 1)
            wo = [w_get(), w_get()]
            for t in range(nt):
                for cg in range(2):
                    bk = nb()
                    for kc in range(8):
                        P.mm(bk[:, :], mT[:, kc, t * 128:(t + 1) * 128], wo[cg][:, kc, :],
                             start=(kc == 0), stop=(kc == 7))
                    xs_ = xres[:, t, cg * 512:(cg + 1) * 512]
                    P.stt("dve", xs_, xs_, ALPHA, bk[:], ALU.mult, ALU.add)
                ln_stats(xres[:, t, :], t)
            ln_rstd(nt)
            for t in range(nt):
                ln_apply(xres[:, t, :], 0, t)
                transpose_tile(xres[:, t, :], hT, t * 128)
            need_conv = b >= 7
            if is_s:
                segs = [(0, 64, 0), (64, 64, 1)]
            else:
                segs = [(0, 512, None)]
            for ci in range(6):
                w = 512 if ci < 5 else 256
                wu = w_get()
                wg_ = w_get()
                if need_conv:
                    bk = nb()
                    for kc in range(8):
                        P.mm(bk[:, :w], hT[:, kc, n - 128:n], wu[:, kc, 0:w], start=(kc == 0), stop=(kc == 7))
                    P.copy("dve", utok[:, ci * 512:ci * 512 + w], bk[:, :w])
                for fl in range(w // 128):
                    fo = ci * 4 + fl
                    c = slice(fl * 128, fl * 128 + 128)
                    bu = nb()
                    for kc in range(8):
                        P.mm(bu[:, :n], wu[:, kc, c], hT[:, kc, 0:n], start=(kc == 0), stop=(kc == 7))
                    bg = nb()
                    for kc in range(8):
                        P.mm(bg[:, :n], wg_[:, kc, c], hT[:, kc, 0:n], start=(kc == 0), stop=(kc == 7))
                    uo = 0
                    for (sc0, sl, ss) in segs:
                        ub = ubuf[:, uo:uo + sl + 2]
                        uo += sl + 2
                        halo = hal[:, fo, :] if ss is None else cch[ss][:, fo, :]
                        P.copy("pool", ub[:, 0:2], halo)
                        P.copy("act", ub[:, 2:2 + sl], bu[:, sc0:sc0 + sl])
                        cv = cva[:, sc0:sc0 + sl]
                        P.ts("dve", cv, ub[:, 0:sl], convp_s[:, fo, 0:1], convp_s[:, fo, 3:4], ALU.mult, ALU.add)
                        P.stt("dve", cv, ub[:, 1:1 + sl], convp_s[:, fo, 1:2], cv, ALU.mult, ALU.add)
                        P.stt("dve", cv, ub[:, 2:2 + sl], convp_s[:, fo, 2:3], cv, ALU.mult, ALU.add)
                        if ss is None and b < NBLK - 1:
                            P.copy("pool", hal[:, fo, :], ub[:, sl:sl + 2])
                    P.act(cvg[:, :n], cva[:, :n], AF.Gelu_apprx_tanh)
                    P.tt("dve", zT[:, fo, 0:n], cvg[:, :n], bg[:, :n], ALU.mult)
            if need_conv:
                if is_s:
                    for s in range(2):
                        P.dma("sp", oconv_s[2 * s:2 * s + 2, :], utok[64 * s + 62:64 * s + 64, :])
                else:
                    P.dma("sp", oconv_p.ap(), utok[126:128, :])
            for cg in range(2):
                banks = [nb() for _ in range(nt)]
                for kg in range(3):
                    wd = w_get()
                    nkc = 8 if kg < 2 else 6
                    for t in range(nt):
                        for kl in range(nkc):
                            kc = kg * 8 + kl
                            P.mm(banks[t][:, :], zT[:, kc, t * 128:(t + 1) * 128], wd[:, kl, :],
                                 start=(kc == 0), stop=(kc == NFC - 1))
                for t in range(nt):
                    xs_ = xres[:, t, cg * 512:(cg + 1) * 512]
                    P.stt("dve", xs_, xs_, ALPHA, banks[t][:], ALU.mult, ALU.add)
                    if cg == 1:
                        ln_stats(xres[:, t, :], t)
            ln_rstd(nt)
            for t in range(nt):
                ln_apply(xres[:, t, :], 2, t)
            P.dma("sp", ydst.rearrange("(t p) d -> p t d", p=128), xres[:, 0:nt, :])

        P.emit()
        build_nc.stats = (len(P.ops), P.n_waits)
    return nc


def _t5_bucket(rel):
    nb_, me = 16, 8
    ret = np.where(rel > 0, nb_, 0)
    n = np.abs(rel)
    nf = np.maximum(n, 1).astype(np.float32)
    large = me + (np.log(nf / np.float32(me)) / np.float32(math.log(128 / me)) * np.float32(nb_ - me)).astype(np.int32)
    large = np.minimum(large, nb_ - 1)
    return ret + np.where(n < me, n, large)


_NC_CACHE = {}


def kernel(**inputs):
    f32 = np.float32
    g = {k: np.asarray(v) for k, v in inputs.items()}
    p = np.arange(128)[:, None]
    mcol = np.arange(256)[None, :]
    rel = p - mcol
    ga = g["t5_table"][_t5_bucket(rel)]
    gb = g["rel_table_b"][0][np.clip(rel, -128, 128) + 128]
    gtab = np.ascontiguousarray(np.concatenate([ga, gb], axis=2).transpose(0, 2, 1)).astype(f32)
    ca = g["t5_table"][_t5_bucket(np.array([-1000]))][0]
    cb = g["rel_table_b"][0][0]
    ctab = np.ascontiguousarray(np.broadcast_to(np.concatenate([ca, cb])[None, :], (128, 24))).astype(f32)
    lamv = np.stack([g["lambda_q1"][0], g["lambda_k1"][0], g["lambda_q2"][0], g["lambda_k2"][0]], 0)
    lamv = np.ascontiguousarray(np.broadcast_to(lamv[None], (128, 4, 64))).astype(f32)
    sublng = np.ascontiguousarray(g["subln_g"][0].reshape(128, 1)).astype(f32)
    lntab = np.stack([g["ln1_g"][0], g["ln1_b"][0], g["ln2_g"][0], g["ln2_b"][0]], 0)
    lntab = np.ascontiguousarray(np.broadcast_to(lntab[None], (128, 4, D))).astype(f32)
    cvp = np.concatenate([g["conv_w"][0], g["conv_b"][0][None]], 0)
    convp = np.ascontiguousarray(cvp.reshape(4, NFC, 128).transpose(2, 1, 0)).astype(f32)
    idn = np.eye(128, dtype=f32)
    shared = {
        "w_in": np.ascontiguousarray(g["w_in"][0]), "w_pa": np.ascontiguousarray(g["w_pa"][0]),
        "w_pb": np.ascontiguousarray(g["w_pb"][0]), "w_out": np.ascontiguousarray(g["w_out"][0]),
        "w_up": np.ascontiguousarray(g["w_up"][0]), "w_gate": np.ascontiguousarray(g["w_gate"][0]),
        "w_down": np.ascontiguousarray(g["w_down"][0]),
        "lamv": lamv, "sublng": sublng, "lntab": lntab, "convp": convp, "gtab": gtab, "ctab": ctab, "idn": idn,
    }
    in_maps = []
    for c in range(N_CORES):
        sl = slice(2 * c, 2 * c + 2)
        m = dict(shared)
        m["xp"] = np.ascontiguousarray(g["x_prompt"][c])
        m["xs"] = np.ascontiguousarray(g["x_sample"][sl].reshape(2 * TS, D))
        m["cak"] = np.ascontiguousarray(g["cache_a_k"][0, sl].reshape(2, 2048, D))
        m["cav"] = np.ascontiguousarray(g["cache_a_v"][0, sl].reshape(2, 2048, D))
        m["cbk"] = np.ascontiguousarray(g["cache_b_k"][0, sl].reshape(2, 512, D))
        m["cbv"] = np.ascontiguousarray(g["cache_b_v"][0, sl].reshape(2, 512, D))
        m["cconv"] = np.ascontiguousarray(g["cache_conv"][0, sl])
        in_maps.append(m)
    if "nc" not in _NC_CACHE:
        _NC_CACHE["nc"] = build_nc()
    nc = _NC_CACHE["nc"]
    res = run_bass_kernel_spmd(nc, in_maps, core_ids=list(range(N_CORES)))
    R = res.results

    def cat(name):
        return np.stack([np.asarray(R[c][name]) for c in range(N_CORES)], 0)

    y_p = cat("y_p")
    y_s = cat("y_s").reshape(16, TS, D)
    oak_p = cat("oak_p").reshape(1, 8, S, 8, 2, 64)
    oav_p = cat("oav_p").reshape(1, 8, S, 8, 128)
    obk_p = cat("obk_p").reshape(1, 8, 512, 16, 64)
    obv_p = cat("obv_p").reshape(1, 8, 512, 16, 64)
    oconv_p = cat("oconv_p").reshape(1, 8, 2, DFF)
    oak_s = cat("oak_s").reshape(1, 16, TS, 8, 2, 64)
    oav_s = cat("oav_s").reshape(1, 16, TS, 8, 128)
    obk_s = cat("obk_s").reshape(1, 16, TS, 16, 64)
    obv_s = cat("obv_s").reshape(1, 16, TS, 16, 64)
    oconv_s = cat("oconv_s").reshape(1, 16, 2, DFF)
    outs = (y_p, y_s, oak_p, oav_p, obk_p, obv_p, oconv_p, oak_s, oav_s, obk_s, obv_s, oconv_s)
    return tuple(np.ascontiguousarray(o, dtype=f32) for o in outs)
```
